# Optimizing a Trainium2 kernel written in Bass

```python
import math
import jax
import jax.numpy as jnp
from jax import lax
import numpy as np

D_MODEL = 1024
BATCH = 2
SEQ = 16384
DEPTH = 4

GRID_W = 64
CTX_LEN = 256
N_MIXERS = 3
N_A = (DEPTH + 2) // 3
N_B = (DEPTH + 1) // 3
N_C = DEPTH // 3
D_FF = 2816
N_MOD = 9
ATTN_HEADS = 8
ATTN_SUB_DIM = D_MODEL // ATTN_HEADS // 2
ATTN_V_DIM = 2 * ATTN_SUB_DIM
Q_BLOCK = 128
ROPE_THETA = 10000.0
HY_EMB_DIM = 33
HY_BANDS = (HY_EMB_DIM - 1) // 2
HY_FILTER_HIDDEN = 64
HY_SHORT_W = 3
HY_FAST_DECAY = 0.3
HY_SLOW_DECAY = 1.5
HY_DECAY_TARGET = 1e-2
CV_WIDTH = 31
EPS = 1e-6
LN_EPS = 1e-5

kernel_name = 'hybrid_diffattn_hyena_conformer_dit'


def rmsnorm(x, g):
    x32 = x.astype(jnp.float32)
    y = x32 * lax.rsqrt(jnp.mean(x32 * x32, axis=-1, keepdims=True) + EPS)
    return y.astype(x.dtype) * g


def layernorm(x, g, b):
    x32 = x.astype(jnp.float32)
    mu = jnp.mean(x32, axis=-1, keepdims=True)
    var = jnp.mean(jnp.square(x32 - mu), axis=-1, keepdims=True)
    return ((x32 - mu) * lax.rsqrt(var + LN_EPS)).astype(x.dtype) * g + b


def modulate(x, g, shift, scale):
    return rmsnorm(x, g) * (1.0 + scale) + shift


def swiglu(h, w_in, w_out):
    gate, up = jnp.split(h @ w_in, 2, axis=-1)
    return (jax.nn.silu(gate) * up) @ w_out


def depthwise_conv(x, w, b):
    pad = (w.shape[0] - 1) // 2
    y = lax.conv_general_dilated(x, w[:, None, :].astype(x.dtype), window_strides=(1,),
                                 padding=[(pad, pad)], dimension_numbers=('NWC', 'WIO', 'NWC'),
                                 feature_group_count=x.shape[-1])
    return y + b


def axial_rope_tables(n):
    rows = n // GRID_W
    row = jnp.repeat(jnp.arange(rows), GRID_W).astype(jnp.float32)
    col = jnp.tile(jnp.arange(GRID_W), rows).astype(jnp.float32)
    half = ATTN_SUB_DIM // 2
    quarter = half // 2
    inv = ROPE_THETA ** (-(2.0 * jnp.arange(quarter, dtype=jnp.float32)) / half)
    ang = jnp.concatenate([row[:, None] * inv, col[:, None] * inv], axis=-1)
    return jnp.cos(ang), jnp.sin(ang)


def apply_axial_rope(t, cos, sin):
    q = t.shape[-1] // 4
    cs = cos[:, None, None, :]
    sn = sin[:, None, None, :]

    def rot(seg, c_, s_):
        lo, hi = seg[..., :q], seg[..., q:]
        return jnp.concatenate([lo * c_ - hi * s_, lo * s_ + hi * c_], axis=-1)

    return jnp.concatenate([rot(t[..., :2 * q], cs[..., :q], sn[..., :q]),
                            rot(t[..., 2 * q:], cs[..., q:], sn[..., q:])], axis=-1)


def diff_attention(h_lat, h_ctx, w_qkv, w_o, lam_vecs, subln_g, lam_init, need_ctx_out):
    bsz, n_lat, _ = h_lat.shape
    out_dtype = h_lat.dtype

    def project(h):
        n = h.shape[1]
        q, k, v = jnp.split((h @ w_qkv).astype(jnp.float32), 3, axis=-1)
        return (q.reshape(bsz, n, ATTN_HEADS, 2, ATTN_SUB_DIM),
                k.reshape(bsz, n, ATTN_HEADS, 2, ATTN_SUB_DIM),
                v.reshape(bsz, n, ATTN_HEADS, ATTN_V_DIM))

    lv = lam_vecs.astype(jnp.float32)
    lam = jnp.exp(jnp.sum(lv[0] * lv[1])) - jnp.exp(jnp.sum(lv[2] * lv[3])) + lam_init
    scale = ATTN_SUB_DIM ** -0.5

    def attend(q, k, v):
        s = jnp.einsum('bqhmd,bkhmd->bhmqk', q * scale, k)
        p = jax.nn.softmax(s, axis=-1)
        w = p[:, :, 0] - lam * p[:, :, 1]
        return jnp.einsum('bhqk,bkhe->bqhe', w, v)

    def merge(o):
        n = o.shape[1]
        o = rmsnorm(o, subln_g) * (1.0 - lam_init)
        return o.reshape(bsz, n, D_MODEL).astype(out_dtype) @ w_o

    q_l, k_l, v_l = project(h_lat)
    cos, sin = axial_rope_tables(n_lat)
    q_l = apply_axial_rope(q_l, cos, sin)
    k_l = apply_axial_rope(k_l, cos, sin)
    q_c, k_c, v_c = project(h_ctx)
    k_all = jnp.concatenate([k_l, k_c], axis=1)
    v_all = jnp.concatenate([v_l, v_c], axis=1)
    n_blk = n_lat // Q_BLOCK
    q_blk = jnp.moveaxis(q_l.reshape(bsz, n_blk, Q_BLOCK, ATTN_HEADS, 2, ATTN_SUB_DIM), 1, 0)
    o_blk = lax.map(lambda qb: attend(qb, k_all, v_all), q_blk)
    o_l = jnp.moveaxis(o_blk, 0, 1).reshape(bsz, n_lat, ATTN_HEADS, ATTN_V_DIM)
    y_lat = merge(o_l)
    y_ctx = merge(attend(q_c, k_c, v_c)) if need_ctx_out else None
    return y_lat, y_ctx


def hyena_filter(L, w1, b1, w2, b2, w3, b3, freq, w4):
    f32 = jnp.float32
    t = jnp.linspace(0.0, 1.0, L, dtype=f32)[:, None]
    wpos = (2.0 * math.pi / L) * jnp.arange(L, dtype=f32)
    bands = jnp.linspace(1e-4, HY_BANDS - 1, HY_BANDS, dtype=f32)
    fw = wpos[:, None] * bands[None, :]
    emb = jnp.concatenate([t, jnp.cos(fw), -jnp.sin(fw)], axis=-1)
    fr = freq.astype(f32)
    hdn = jnp.sin(fr * (emb @ w1.astype(f32) + b1.astype(f32)))
    hdn = jnp.sin(fr * (hdn @ w2.astype(f32) + b2.astype(f32)))
    hdn = jnp.sin(fr * (hdn @ w3.astype(f32) + b3.astype(f32)))
    h = hdn @ w4.astype(f32)
    max_decay = math.log(HY_DECAY_TARGET) / HY_FAST_DECAY
    min_decay = math.log(HY_DECAY_TARGET) / HY_SLOW_DECAY
    deltas = jnp.abs(jnp.linspace(min_decay, max_decay, D_MODEL, dtype=f32))
    decay = jnp.exp(-t * deltas[None, :])
    return h * jnp.concatenate([decay, decay], axis=-1)


def bidir_long_conv(v, h_fwd, h_bwd):
    L = v.shape[1]
    n = 2 * L
    taps = jnp.concatenate([h_fwd, jnp.zeros((1, h_fwd.shape[1]), jnp.float32), h_bwd[:0:-1]], axis=0)
    vf = jnp.fft.rfft(v.astype(jnp.float32), n=n, axis=1)
    hf = jnp.fft.rfft(taps, n=n, axis=0)
    return jnp.fft.irfft(vf * hf[None], n=n, axis=1)[:, :L]


def hyena_mixer(h, w_in, b_in, w_short, b_short, f_w1, f_b1, f_w2, f_b2, f_w3, f_b3, f_freq, f_w4,
                skip_bias, w_out, b_out):
    L = h.shape[1]
    u = depthwise_conv(h @ w_in + b_in, w_short, b_short)
    x0, x1, v = jnp.split(u, 3, axis=-1)
    h_fwd, h_bwd = jnp.split(hyena_filter(L, f_w1, f_b1, f_w2, f_b2, f_w3, f_b3, f_freq, f_w4), 2, axis=-1)
    v = v * x1
    y = bidir_long_conv(v, h_fwd, h_bwd).astype(v.dtype) + v * skip_bias
    return (y * x0) @ w_out + b_out


def conformer_conv(h, w_pw1, b_pw1, w_dw, b_dw, ln_g, ln_b, w_pw2, b_pw2):
    a, g = jnp.split(h @ w_pw1 + b_pw1, 2, axis=-1)
    u = a * jax.nn.sigmoid(g)
    u = depthwise_conv(u, w_dw, b_dw)
    u = jax.nn.silu(layernorm(u, ln_g, ln_b))
    return u @ w_pw2 + b_pw2


def setup_inputs(seed: int = 0) -> dict:
    key = jax.random.key(seed)
    ks = iter(jax.random.split(key, 48))
    f32 = jnp.float32

    def nrm(shape, scale):
        return jax.random.normal(next(ks), shape, f32) * scale

    D = D_MODEL
    return {
        'x': nrm((BATCH, SEQ, D), 1.0),
        'c': nrm((BATCH, D), 1.0),
        'ctx': nrm((BATCH, CTX_LEN, D), 1.0),
        'c_ctx': nrm((D,), 1.0),
        'w_mod': nrm((DEPTH, D, N_MOD * D), 0.5 * D ** -0.5),
        'b_mod': nrm((DEPTH, N_MOD * D), 0.01),
        'norm_g': 1.0 + nrm((DEPTH, 3, D), 0.01),
        'w_ffn_in': nrm((DEPTH, 2, D, 2 * D_FF), D ** -0.5),
        'w_ffn_out': nrm((DEPTH, 2, D_FF, D), D_FF ** -0.5),
        'attn_w_qkv': nrm((N_A, D, 3 * D), D ** -0.5),
        'attn_w_o': nrm((N_A, D, D), D ** -0.5),
        'attn_lambda': nrm((N_A, 4, ATTN_SUB_DIM), 0.1),
        'attn_subln_g': 1.0 + nrm((N_A, ATTN_V_DIM), 0.01),
        'hy_w_in': nrm((N_B, D, 3 * D), D ** -0.5),
        'hy_b_in': nrm((N_B, 3 * D), 0.01),
        'hy_w_short': nrm((N_B, HY_SHORT_W, 3 * D), HY_SHORT_W ** -0.5),
        'hy_b_short': nrm((N_B, 3 * D), 0.01),
        'hy_f_w1': nrm((N_B, HY_EMB_DIM, HY_FILTER_HIDDEN), HY_EMB_DIM ** -0.5),
        'hy_f_b1': nrm((N_B, HY_FILTER_HIDDEN), 0.1),
        'hy_f_w2': nrm((N_B, HY_FILTER_HIDDEN, HY_FILTER_HIDDEN), HY_FILTER_HIDDEN ** -0.5),
        'hy_f_b2': nrm((N_B, HY_FILTER_HIDDEN), 0.1),
        'hy_f_w3': nrm((N_B, HY_FILTER_HIDDEN, HY_FILTER_HIDDEN), HY_FILTER_HIDDEN ** -0.5),
        'hy_f_b3': nrm((N_B, HY_FILTER_HIDDEN), 0.1),
        'hy_f_freq': 1.0 + nrm((N_B, HY_FILTER_HIDDEN), 0.01),
        'hy_f_w4': nrm((N_B, HY_FILTER_HIDDEN, 2 * D), 0.01),
        'hy_skip': nrm((N_B, D), 1.0),
        'hy_w_out': nrm((N_B, D, D), D ** -0.5),
        'hy_b_out': nrm((N_B, D), 0.01),
        'cv_w_pw1': nrm((N_C, D, 2 * D), D ** -0.5),
        'cv_b_pw1': nrm((N_C, 2 * D), 0.01),
        'cv_w_dw': nrm((N_C, CV_WIDTH, D), CV_WIDTH ** -0.5),
        'cv_b_dw': nrm((N_C, D), 0.01),
        'cv_ln_g': 1.0 + nrm((N_C, D), 0.01),
        'cv_ln_b': nrm((N_C, D), 0.01),
        'cv_w_pw2': nrm((N_C, D, D), D ** -0.5),
        'cv_b_pw2': nrm((N_C, D), 0.01),
        'final_g': 1.0 + nrm((D,), 0.01),
    }


def reference(x, c, ctx, c_ctx, w_mod, b_mod, norm_g, w_ffn_in, w_ffn_out,
              attn_w_qkv, attn_w_o, attn_lambda, attn_subln_g,
              hy_w_in, hy_b_in, hy_w_short, hy_b_short, hy_f_w1, hy_f_b1, hy_f_w2, hy_f_b2,
              hy_f_w3, hy_f_b3, hy_f_freq, hy_f_w4, hy_skip, hy_w_out, hy_b_out,
              cv_w_pw1, cv_b_pw1, cv_w_dw, cv_b_dw, cv_ln_g, cv_ln_b, cv_w_pw2, cv_b_pw2,
              final_g):
    silu_c = jax.nn.silu(c)
    silu_cc = jax.nn.silu(c_ctx)
    xc = ctx
    for i in range(DEPTH):
        kind = i % N_MIXERS
        j = i // N_MIXERS
        last = i == DEPTH - 1
        ctx_in_use = (not last) or kind == 0
        ctx_advance = not last
        ml = jnp.split((silu_c @ w_mod[i] + b_mod[i])[:, None, :], N_MOD, axis=-1)
        mc = jnp.split(silu_cc @ w_mod[i] + b_mod[i], N_MOD, axis=-1)

        x = x + 0.5 * ml[2] * swiglu(modulate(x, norm_g[i, 0], ml[0], ml[1]), w_ffn_in[i, 0], w_ffn_out[i, 0])
        if ctx_in_use:
            xc = xc + 0.5 * mc[2] * swiglu(modulate(xc, norm_g[i, 0], mc[0], mc[1]), w_ffn_in[i, 0], w_ffn_out[i, 0])

        hl = modulate(x, norm_g[i, 1], ml[3], ml[4])
        hc = modulate(xc, norm_g[i, 1], mc[3], mc[4]) if ctx_in_use else None
        if kind == 0:
            lam_init = 0.8 - 0.6 * math.exp(-0.3 * i)
            yl, yc = diff_attention(hl, hc, attn_w_qkv[j], attn_w_o[j], attn_lambda[j], attn_subln_g[j],
                                    lam_init, ctx_advance)
        elif kind == 1:
            hy = (hy_w_in[j], hy_b_in[j], hy_w_short[j], hy_b_short[j], hy_f_w1[j], hy_f_b1[j],
                  hy_f_w2[j], hy_f_b2[j], hy_f_w3[j], hy_f_b3[j], hy_f_freq[j], hy_f_w4[j],
                  hy_skip[j], hy_w_out[j], hy_b_out[j])
            yl = hyena_mixer(hl, *hy)
            yc = hyena_mixer(hc, *hy) if ctx_advance else None
        else:
            cv = (cv_w_pw1[j], cv_b_pw1[j], cv_w_dw[j], cv_b_dw[j], cv_ln_g[j], cv_ln_b[j],
                  cv_w_pw2[j], cv_b_pw2[j])
            yl = conformer_conv(hl, *cv)
            yc = conformer_conv(hc, *cv) if ctx_advance else None
        x = x + ml[5] * yl

        x = x + 0.5 * ml[8] * swiglu(modulate(x, norm_g[i, 2], ml[6], ml[7]), w_ffn_in[i, 1], w_ffn_out[i, 1])
        if ctx_advance:
            xc = xc + mc[5] * yc
            xc = xc + 0.5 * mc[8] * swiglu(modulate(xc, norm_g[i, 2], mc[6], mc[7]), w_ffn_in[i, 1], w_ffn_out[i, 1])
    return rmsnorm(x, final_g)
```

```python
from contextlib import ExitStack
import math
import numpy as np
import concourse.bass as bass
import concourse.mybir as mybir
from concourse.bass_utils import run_bass_kernel_spmd

F32 = mybir.dt.float32
BF16 = mybir.dt.bfloat16
AF = mybir.ActivationFunctionType
ALU = mybir.AluOpType
NDMA_SEMS = 64
DRAM_SHAPES = {}
NSW_SEMS = 2

D = 1024
KC = 8
B = 2
SEQ = 16384
GRID_W = 64
CTX = 256
DFF = 2816
FC = DFF // 128
NCORE = 8
TOK_CTX = CTX * B // NCORE
EPS = 1e-6


class Buf:
    __slots__ = ("t", "last_w", "readers", "name")

    def __init__(self, t, name=""):
        self.t = t
        self.last_w = None
        self.readers = []
        self.name = name

    def __getitem__(self, idx):
        return self.t[idx]


class Ring:
    def __init__(self, bufs):
        self.bufs = bufs
        self.i = 0

    def next(self):
        b = self.bufs[self.i % len(self.bufs)]
        self.i += 1
        return b


class KB:
    def __init__(self):
        self.nc = bass.Bass("TRN2", target_bir_lowering=False)
        self.es = ExitStack()
        nc = self.nc
        self.eng = {"pe": nc.tensor, "act": nc.scalar, "dve": nc.vector, "pool": nc.gpsimd, "sp": nc.sync}
        self.csem = {}
        self.ccnt = {}
        for e in ("pe", "act", "dve", "pool"):
            self.csem[e] = self.es.enter_context(nc.semaphore("c_" + e))
            self.ccnt[e] = 0
        self.dsem = [self.es.enter_context(nc.semaphore(f"d{i}")) for i in range(NDMA_SEMS)]
        self.dval = [0] * NDMA_SEMS
        self.dtok = [None] * NDMA_SEMS
        self.dnext = 0
        self.wsem = [self.es.enter_context(nc.semaphore(f"w{i}")) for i in range(NSW_SEMS)]
        self.wtok = [None] * NSW_SEMS
        self.wgen = [0] * NSW_SEMS
        self.wnext = 0
        self.cur_sw = []
        self.sw_waiters = {}
        self.waited = {e: {} for e in self.eng}
        self.nbuf = 0
        self.psum_banks = []
        self.psum_i = 0
        self.out_toks = []
        self.cast_i = 0
        self.psum_all = self.es.enter_context(nc.psum_tensor("psall", [128, 4096], F32))
        for i in range(8):
            self.psum_banks.append(Buf(self.psum_all[:, i * 512:(i + 1) * 512], f"ps{i}"))

    def sbt(self, shape, dt, name=None):
        self.nbuf += 1
        name = "s_" + (name or f"b{self.nbuf}")
        return self.es.enter_context(self.nc.sbuf_tensor(name, list(shape), dt))

    def sb(self, shape, dt, name=None):
        t = self.sbt(shape, dt, name)
        return Buf(t, name or "")

    def ring(self, n, shape, dt, name=None):
        self.nbuf += 1
        name = name or f"r{self.nbuf}"
        return Ring([self.sb(shape, dt, f"{name}_{i}") for i in range(n)])

    @property
    def stg(self):
        if not hasattr(self, "_stg"):
            self._stg = self.ring(3, [128, STG_N], F32, "stg")
        return self._stg

    def psum(self):
        b = self.psum_banks[self.psum_i % 8]
        self.psum_i += 1
        return b

    def dram(self, name, shape, dt, kind="ExternalInput"):
        DRAM_SHAPES.setdefault(id(self.nc), {})[name] = (list(shape), dt, kind)
        return self.nc.dram_tensor(name, list(shape), dt, kind=kind).ap()

    def _wait(self, e, tok):
        if tok is None:
            return
        sem, val, key = tok
        if e == "pe" and key == "c_pe":
            return
        if self.waited[e].get(key, 0) >= val:
            return
        if key[0] == "w":
            k, gen = key[1:].split("_")
            if self.wgen[int(k)] != int(gen):
                raise RuntimeError(f"stale SW-DMA token {key} waited by {e}")
            self.cur_sw.append(key)
        self.eng[e].wait_ge(sem, val)
        self.waited[e][key] = val

    def _deps(self, e, R, W):
        for b in R:
            self._wait(e, b.last_w)
        for b in W:
            self._wait(e, b.last_w)
            for tk in b.readers:
                self._wait(e, tk)

    def _commit(self, tok, R, W):
        for b in R:
            b.readers.append(tok)
            if len(b.readers) > 16:
                d = {}
                for tk in b.readers:
                    if tk[2] not in d or d[tk[2]][1] < tk[1]:
                        d[tk[2]] = tk
                b.readers = list(d.values())
        for b in W:
            b.last_w = tok
            b.readers = []

    def op(self, e, fn, R=(), W=()):
        self._deps(e, R, W)
        inst = fn(self.eng[e])
        self.ccnt[e] += 1
        inst.then_inc(self.csem[e], 1)
        tok = (self.csem[e], self.ccnt[e], "c_" + e)
        self._reg_waiters(e, tok)
        self._commit(tok, R, W)
        return tok

    def _reg_waiters(self, e, tok):
        for key in self.cur_sw:
            self.sw_waiters.setdefault(key, {})[e] = tok
        self.cur_sw = []

    def dma(self, q, out, in_, R=(), W=(), is_output=False, **kw):
        if is_output and q == "sp":
            q = "act"
        if q == "pool":
            k = self.wnext
            self.wnext = (self.wnext + 1) % NSW_SEMS
            if self.wtok[k] is not None:
                self._wait(q, self.wtok[k])
                for we, wt in self.sw_waiters.pop(self.wtok[k][2], {}).items():
                    if we != "pool":
                        self._wait(q, wt)
                self.cur_sw = [x for x in self.cur_sw if x != self.wtok[k][2]]
                self.wgen[k] += 1
                self.eng[q].sem_clear(self.wsem[k])
            self._deps(q, R, W)
            inst = self.eng[q].dma_start(out=out, in_=in_, **kw)
            inst.then_inc(self.wsem[k], 16)
            tok = (self.wsem[k], 16, f"w{k}_{self.wgen[k]}")
            self.wtok[k] = tok
        else:
            k = self.dnext
            self.dnext = (self.dnext + 1) % NDMA_SEMS
            self._wait(q, self.dtok[k])
            self._deps(q, R, W)
            inst = self.eng[q].dma_start(out=out, in_=in_, **kw)
            self.dval[k] += 16
            inst.then_inc(self.dsem[k], 16)
            tok = (self.dsem[k], self.dval[k], f"d{k}")
            self.dtok[k] = tok
        self._reg_waiters(q, tok)
        self._commit(tok, R, W)
        if is_output:
            self.out_toks.append(tok)
        return tok

    def finish(self):
        for tok in self.out_toks:
            self._wait("sp", tok)
        for k in range(NDMA_SEMS):
            self._wait("sp", self.dtok[k])
        for k in range(NSW_SEMS):
            self._wait("sp", self.wtok[k])
        for e in ("pe", "act", "dve", "pool"):
            if self.ccnt[e]:
                self._wait("sp", (self.csem[e], self.ccnt[e], "c_" + e))
        self.es.close()
        return self.nc


def tile_w(W, order=None):
    K, N = W.shape
    kc = K // 128
    J = N // 128
    Wt = W.reshape(kc, 128, J, 128).transpose(2, 1, 0, 3)
    if order is not None:
        Wt = Wt[np.asarray(order)]
    return np.ascontiguousarray(Wt, dtype=np.float32)


def fm_vec(v):
    v = np.asarray(v, dtype=np.float32)
    lead = v.shape[:-1]
    n = v.shape[-1] // 128
    vv = v.reshape(lead + (n, 128))
    vv = np.moveaxis(vv, -1, 0)
    return np.ascontiguousarray(vv)


class Consts:
    pass


STG_N = 3072
CAST_ENGINES = ("act", "dve", "pool", "act", "dve")


def stage_cast(kb, dst_ap, src_ap, pat, W, **dims):
    st = kb.stg.next()
    n = 1
    for d in dst_ap.shape[1:]:
        n *= d
    assert n <= STG_N, n
    view = st.t[:dst_ap.shape[0], 0:n]
    if pat is not None:
        view = view.rearrange(pat, **dims)
    kb.dma("sp", view, src_ap, W=[st])
    eng = CAST_ENGINES[kb.cast_i % len(CAST_ENGINES)]
    kb.cast_i += 1
    if eng == "act":
        kb.op("act", lambda e: e.activation(out=dst_ap, in_=view, func=AF.Copy), R=[st], W=W)
    else:
        kb.op(eng, lambda e: e.tensor_copy(out=dst_ap, in_=view), R=[st], W=W)


def emit_consts(kb):
    c = Consts()
    c.ones32 = kb.sb([128, 128], F32, "ones32")
    kb.op("dve", lambda e: e.memset(c.ones32[:, :], 1.0), W=[c.ones32])
    c.eps = kb.sb([128, 1], F32, "epsc")
    kb.op("dve", lambda e: e.memset(c.eps[:, :], EPS), W=[c.eps])
    c.ones16 = kb.sb([128, 128], BF16, "ones16")
    kb.op("dve", lambda e: e.memset(c.ones16[:, :], 1.0), W=[c.ones16])
    return c


def emit_mod(kb, cst, cT_d, wmod_d, bmod_d, normg_d, name):
    sc = kb.sb([128, KC, 2], F32, name + "_sc")
    kb.dma("sp", sc[:, :, :], cT_d, W=[sc])
    sil = kb.sb([128, KC, 2], F32, name + "_sil")
    kb.op("act", lambda e: e.activation(out=sil[:, :, :], in_=sc[:, :, :], func=AF.Silu), R=[sc], W=[sil])
    bm = kb.sb([128, 72], F32, name + "_bm")
    kb.dma("sp", bm[:, :], bmod_d, W=[bm])
    ng = kb.sb([128, 3, KC], F32, name + "_ng")
    kb.dma("sp", ng[:, :, :], normg_d, W=[ng])
    if not hasattr(kb, "_modring"):
        kb._modring = kb.ring(3, [128, KC, 128], F32, "modw")
    wring = kb._modring
    ps = kb.psum()
    first = True
    for j in range(72):
        wb = wring.next()
        kb.dma("sp", wb[:, :, :], wmod_d[j], W=[wb])

        def mm(e, wb=wb, j=j):
            for kc in range(KC):
                i = e.matmul(ps[:, 2 * j:2 * j + 2], wb[:, kc, :], sil[:, kc, :],
                             start=(kc == 0), stop=(kc == KC - 1))
            return i
        kb.op("pe", mm, R=[wb, sil], W=[ps])
    mod = kb.sb([128, 72, 2], F32, name + "_mod")
    kb.op("dve", lambda e: e.tensor_tensor(
        out=mod[:, :, :], in0=ps[:, 0:144].rearrange("p (j r) -> p j r", r=2),
        in1=bm[:, :, None].broadcast_to([128, 72, 2]), op=ALU.add), R=[ps, bm], W=[mod])
    tabs = {"A": [], "Bv": [], "G": [], "mod": mod}
    for s in range(3):
        A = kb.sb([128, KC, 2], F32, f"{name}_A{s}")
        kb.op("dve", lambda e, A=A, s=s: e.scalar_tensor_tensor(
            out=A[:, :, :], in0=mod[:, (3 * s + 1) * 8:(3 * s + 2) * 8, :], scalar=1.0,
            in1=ng[:, s, :, None].broadcast_to([128, KC, 2]), op0=ALU.add, op1=ALU.mult),
            R=[mod, ng], W=[A])
        G = kb.sb([128, KC, 2], F32, f"{name}_G{s}")
        gs = 1.0 if s == 1 else 0.5
        kb.op("dve", lambda e, G=G, s=s, gs=gs: e.tensor_scalar(
            out=G[:, :, :], in0=mod[:, (3 * s + 2) * 8:(3 * s + 3) * 8, :], scalar1=gs, scalar2=None,
            op0=ALU.mult), R=[mod], W=[G])
        tabs["A"].append(A)
        tabs["G"].append(G)
        tabs["Bv"].append((mod, (3 * s) * 8))
    return tabs


class TokBufs:
    def __init__(self, kb):
        self.xt = kb.sbt([128, KC, 512], F32, "xt")
        self.x = [Buf(self.xt, f"x{k}") for k in range(KC)]
        self.ht = kb.sbt([128, KC, 512], BF16, "ht")
        self.h = [Buf(self.ht, f"h{k}") for k in range(KC)]
        self.at = kb.sbt([128, FC, 512], BF16, "at")
        self.a = [Buf(self.at, f"a{k}") for k in range(FC)]
        self.sq = kb.ring(2, [128, 512], F32, "sq")
        self.rstd = kb.sb([128, 512], F32, "rstd")
        self.tmp = kb.ring(3, [128, 512], F32, "tmp")
        self.w8 = kb.ring(3, [128, 2, KC, 128], BF16, "w8")
        self.w22 = kb.ring(3, [128, FC, 128], BF16, "w22")


def emit_norm_mod(kb, cst, tb, T, A, Bv, r):
    ps = kb.psum()
    for kc in range(KC):
        sq = tb.sq.next()
        kb.op("act", lambda e, sq=sq, kc=kc: e.activation(out=sq[:, :T], in_=tb.xt[:, kc, :T], func=AF.Square),
              R=[tb.x[kc]], W=[sq])
        kb.op("pe", lambda e, sq=sq, kc=kc: e.matmul(ps[:, :T], cst.ones32[:, :], sq[:, :T],
                                                      start=(kc == 0), stop=(kc == KC - 1)),
              R=[sq, cst.ones32], W=[ps])
    kb.op("act", lambda e: e.activation(out=tb.rstd[:, :T], in_=ps[:, :T], func=AF.Sqrt, scale=1.0 / D, bias=cst.eps[:, 0:1]),
          R=[ps, cst.eps], W=[tb.rstd])
    kb.op("dve", lambda e: e.reciprocal(out=tb.rstd[:, :T], in_=tb.rstd[:, :T]), R=[tb.rstd], W=[tb.rstd])
    modb, base = Bv
    for kc in range(KC):
        tmp = tb.tmp.next()
        kb.op("dve", lambda e, tmp=tmp, kc=kc: e.tensor_tensor(out=tmp[:, :T], in0=tb.xt[:, kc, :T],
                                                                in1=tb.rstd[:, :T], op=ALU.mult),
              R=[tb.x[kc], tb.rstd], W=[tmp])
        kb.op("act", lambda e, tmp=tmp, kc=kc: e.activation(
            out=tb.ht[:, kc, :T], in_=tmp[:, :T], func=AF.Identity,
            scale=A[:, kc, r:r + 1], bias=modb[:, base + kc, r:r + 1]),
            R=[tmp, A, modb], W=[tb.h[kc]])


def emit_proj(kb, tb, T, w_d, J, epi, group=2):
    slabs = [(j0, min(group, J - j0)) for j0 in range(0, J, group)]
    loaded = {}

    def issue(i):
        j0, g = slabs[i]
        wb = tb.w8.next()
        stage_cast(kb, wb[:, 0:g, :, :], w_d[j0:j0 + g].rearrange("j p k n -> p j k n"), "p (j k n) -> p j k n", [wb], j=g, k=KC)
        loaded[i] = wb
    for i in range(min(2, len(slabs))):
        issue(i)
    for i, (j0, g) in enumerate(slabs):
        if i + 2 < len(slabs):
            issue(i + 2)
        wb = loaded.pop(i)
        pss = []
        for gi in range(g):
            ps = kb.psum()

            def mm(e, ps=ps, gi=gi, wb=wb):
                for kc in range(KC):
                    i_ = e.matmul(ps[:, :T], wb[:, gi, kc, :], tb.ht[:, kc, :T], start=(kc == 0), stop=(kc == KC - 1))
                return i_
            kb.op("pe", mm, R=[wb] + tb.h, W=[ps])
            pss.append(ps)
        epi(j0, pss)


def emit_ffn(kb, cst, tb, T, tabs, s, r, win_d, wout_d):
    emit_norm_mod(kb, cst, tb, T, tabs["A"][s], tabs["Bv"][s], r)

    def epi(j0, pss):
        j = j0 // 2
        sg = tb.tmp.next()
        kb.op("act", lambda e: e.activation(out=sg[:, :T], in_=pss[0][:, :T], func=AF.Silu), R=[pss[0]], W=[sg])
        kb.op("dve", lambda e: e.tensor_tensor(out=tb.at[:, j, :T], in0=pss[1][:, :T], in1=sg[:, :T], op=ALU.mult),
              R=[pss[1], sg], W=[tb.a[j]])
    emit_proj(kb, tb, T, win_d, 2 * FC, epi, group=2)
    G = tabs["G"][s]
    loaded = {}

    def issue(m):
        wb = tb.w22.next()
        stage_cast(kb, wb[:, :, :], wout_d[m], "p (k n) -> p k n", [wb], k=FC)
        loaded[m] = wb
    issue(0)
    issue(1)
    for m in range(KC):
        if m + 2 < KC:
            issue(m + 2)
        wb = loaded.pop(m)
        ps = kb.psum()

        def mm(e, ps=ps, wb=wb):
            for k in range(FC):
                i = e.matmul(ps[:, :T], wb[:, k, :], tb.at[:, k, :T], start=(k == 0), stop=(k == FC - 1))
            return i
        kb.op("pe", mm, R=[wb] + tb.a, W=[ps])
        kb.op("dve", lambda e, ps=ps, m=m: e.scalar_tensor_tensor(
            out=tb.xt[:, m, :T], in0=ps[:, :T], scalar=G[:, m, r:r + 1], in1=tb.xt[:, m, :T],
            op0=ALU.mult, op1=ALU.add), R=[ps, G, tb.x[m]], W=[tb.x[m]])


def emit_load_h(kb, tb, src_d, t0, T):
    v = src_d.rearrange("(kc p) t -> p kc t", p=128)
    for k0 in range(0, KC, 4):
        stage_cast(kb, tb.ht[:, k0:k0 + 4, :T], v[:, k0:k0 + 4, t0:t0 + T], "p (k t) -> p k t", tb.h[k0:k0 + 4], k=4)


def emit_post_proj(kb, tb, T, w_d, G, r, GB=None):
    if GB is not None:
        kb.op("dve", lambda e: e.tensor_tensor(
            out=tb.xt[:, :, :T], in0=tb.xt[:, :, :T], in1=GB[:, :, r:r + 1].broadcast_to([128, KC, T]), op=ALU.add),
            R=tb.x + [GB], W=tb.x)

    def epi(j0, pss):
        for gi, ps in enumerate(pss):
            m = j0 + gi
            kb.op("dve", lambda e, ps=ps, m=m: e.scalar_tensor_tensor(
                out=tb.xt[:, m, :T], in0=ps[:, :T], scalar=G[:, m, r:r + 1], in1=tb.xt[:, m, :T],
                op0=ALU.mult, op1=ALU.add), R=[ps, G, tb.x[m]], W=[tb.x[m]])
    emit_proj(kb, tb, T, w_d, KC, epi, group=2)


def emit_gb(kb, G, bias, name):
    GB = kb.sb([128, KC, 2], F32, name)
    kb.op("dve", lambda e: e.tensor_tensor(out=GB[:, :, :], in0=G[:, :, :],
                                           in1=bias[:, :, None].broadcast_to([128, KC, 2]), op=ALU.mult),
          R=[G, bias], W=[GB])
    return GB


class QkvBufs:
    def __init__(self, kb, wv_d):
        self.wv = kb.sb([128, KC, D], BF16, "wv")
        for k0 in range(0, KC, 2):
            stage_cast(kb, self.wv[:, k0:k0 + 2, :], wv_d[:, k0:k0 + 2, :], "p (k n) -> p k n", [self.wv], k=2)
        self.ct = kb.ring(2, [128, 512], F32, "ropec")
        self.st = kb.ring(2, [128, 512], F32, "ropes")
        self.qo = kb.ring(3, [128, 512], BF16, "qo")
        self.vo = kb.ring(3, [128, 512], BF16, "vo")


def emit_qkv(kb, tb, qb, T, t0, wqk_d, ctab_d, stab_d, QT_d, KT_d, V_d):
    ct = qb.ct.next()
    st = qb.st.next()
    kb.dma("sp", ct[:, :T], ctab_d[:, t0:t0 + T], W=[ct])
    kb.dma("sp", st[:, :T], stab_d[:, t0:t0 + T], W=[st])

    def epi(j0, pss):
        hd = j0 // 4
        which = (j0 % 4) // 2
        dst = QT_d if which == 0 else KT_d
        pa, pb = pss[0], pss[1]
        t1 = tb.tmp.next()
        kb.op("dve", lambda e: e.tensor_tensor(out=t1[:, :T], in0=pa[:, :T], in1=ct[:, :T], op=ALU.mult),
              R=[pa, ct], W=[t1])
        t2 = tb.tmp.next()
        kb.op("dve", lambda e: e.tensor_tensor(out=t2[:, :T], in0=pb[:, :T], in1=st[:, :T], op=ALU.mult),
              R=[pb, st], W=[t2])
        qo = qb.qo.next()
        kb.op("dve", lambda e: e.tensor_tensor(out=qo[:, :T], in0=t1[:, :T], in1=t2[:, :T], op=ALU.add),
              R=[t1, t2], W=[qo])
        kb.dma("sp", dst[hd, :, t0:t0 + T], qo[:, :T], R=[qo], is_output=True)
    import os as _os
    _m = _os.environ.get("QKV_MODE", "all")
    if _m in ("all", "qk"):
        emit_proj(kb, tb, T, wqk_d, 32, epi, group=2)
    if _m not in ("all", "v"):
        return
    nblk = (T + 127) // 128
    for tblk in range(nblk):
        M = min(128, T - 128 * tblk)
        for half in range(2):
            ps = kb.psum()

            def mm(e, ps=ps, tblk=tblk, M=M, half=half):
                for kc in range(KC):
                    i = e.matmul(ps[:M, :], tb.ht[:, kc, tblk * 128:tblk * 128 + M],
                                 qb.wv[:, kc, half * 512:(half + 1) * 512], start=(kc == 0), stop=(kc == KC - 1))
                return i
            kb.op("pe", mm, R=[qb.wv] + tb.h, W=[ps])
            vo = qb.vo.next()
            kb.op("act", lambda e, ps=ps, vo=vo, M=M: e.activation(out=vo[:M, :], in_=ps[:M, :], func=AF.Copy),
                  R=[ps], W=[vo])
            kb.dma("sp", V_d[t0 + tblk * 128:t0 + tblk * 128 + M, half * 512:(half + 1) * 512], vo[:M, :],
                   R=[vo], is_output=True)


def emit_final_norm(kb, cst, tb, T, fg, out_v, t0):
    ps = kb.psum()
    for kc in range(KC):
        sq = tb.sq.next()
        kb.op("act", lambda e, sq=sq, kc=kc: e.activation(out=sq[:, :T], in_=tb.xt[:, kc, :T], func=AF.Square),
              R=[tb.x[kc]], W=[sq])
        kb.op("pe", lambda e, sq=sq, kc=kc: e.matmul(ps[:, :T], cst.ones32[:, :], sq[:, :T],
                                                      start=(kc == 0), stop=(kc == KC - 1)),
              R=[sq, cst.ones32], W=[ps])
    kb.op("act", lambda e: e.activation(out=tb.rstd[:, :T], in_=ps[:, :T], func=AF.Sqrt, scale=1.0 / D, bias=cst.eps[:, 0:1]),
          R=[ps, cst.eps], W=[tb.rstd])
    kb.op("dve", lambda e: e.reciprocal(out=tb.rstd[:, :T], in_=tb.rstd[:, :T]), R=[tb.rstd], W=[tb.rstd])
    for kc in range(KC):
        kb.op("dve", lambda e, kc=kc: e.scalar_tensor_tensor(
            out=tb.xt[:, kc, :T], in0=tb.xt[:, kc, :T], scalar=fg[:, kc:kc + 1], in1=tb.rstd[:, :T],
            op0=ALU.mult, op1=ALU.mult), R=[tb.x[kc], tb.rstd, fg], W=[tb.x[kc]])
    kb.dma("sp", out_v[:, :, t0:t0 + T], tb.xt[:, :, :T], R=tb.x, is_output=True)


def build_attn(S, lam_init, ctx_out):
    kb = KB()
    NK = S + CTX
    NKB = NK // 128
    NQ = S + (CTX if ctx_out else 0)
    QT = kb.dram("QT", [2, 128, S + CTX], BF16)
    KT = kb.dram("KT", [2, 128, NK], BF16)
    V = kb.dram("V", [2, NK, 128], BF16)
    lamv = kb.dram("lamv", [4, 64], F32)
    subg = kb.dram("subg", [128, 1], F32)
    OT = kb.dram("OT", [2, 128, NQ], F32, kind="ExternalOutput")
    cst = emit_consts(kb)
    eps = kb.sb([128, 1], F32, "eps_a")
    kb.op("dve", lambda e: e.memset(eps[:, :], EPS), W=[eps])
    lv = kb.sb([128, 4, 64], F32, "lv")
    kb.dma("sp", lv[:, :, :], bass.AP(lamv.tensor, 0, [[0, 128], [64, 4], [1, 64]]), W=[lv])
    pr = kb.sb([128, 2, 64], F32, "lpr")
    kb.op("dve", lambda e: e.tensor_tensor(out=pr[:, 0, :], in0=lv[:, 0, :], in1=lv[:, 1, :], op=ALU.mult), R=[lv], W=[pr])
    kb.op("dve", lambda e: e.tensor_tensor(out=pr[:, 1, :], in0=lv[:, 2, :], in1=lv[:, 3, :], op=ALU.mult), R=[lv, pr], W=[pr])
    ss = kb.sb([128, 2], F32, "lss")
    kb.op("dve", lambda e: e.reduce_sum(out=ss[:, :], in_=pr[:, :, :], axis=mybir.AxisListType.X), R=[pr], W=[ss])
    ex = kb.sb([128, 2], F32, "lex")
    kb.op("act", lambda e: e.activation(out=ex[:, :], in_=ss[:, :], func=AF.Exp), R=[ss], W=[ex])
    nlam = kb.sb([128, 1], F32, "nlam")
    kb.op("dve", lambda e: e.tensor_tensor(out=nlam[:, :], in0=ex[:, 1:2], in1=ex[:, 0:1], op=ALU.subtract), R=[ex], W=[nlam])
    kb.op("dve", lambda e: e.tensor_scalar(out=nlam[:, :], in0=nlam[:, :], scalar1=-float(lam_init), scalar2=None, op0=ALU.add),
          R=[nlam], W=[nlam])
    sg = kb.sb([128, 1], F32, "subg_sb")
    kb.dma("sp", sg[:, :], subg, W=[sg])
    gsc = kb.sb([128, 1], F32, "gsc")
    kb.op("dve", lambda e: e.tensor_scalar(out=gsc[:, :], in0=sg[:, :], scalar1=float(1.0 - lam_init), scalar2=None, op0=ALU.mult),
          R=[sg], W=[gsc])

    kt = kb.sb([128, NK], BF16, "kt")
    vt = kb.sb([128, NKB, 128], BF16, "vt")
    qring = kb.ring(2, [128, 512], BF16, "qt")
    pring = kb.ring(4, [128, 1024], BF16, "pt")
    accv = kb.psum_all[:, 2 * 512:4 * 512].rearrange("p (m q) -> p m q", m=2)
    fr = kb.ring(10, [128, 512], F32, "ef")
    banks = kb.psum_banks
    O = [banks[0], banks[1]]
    R_ = [banks[2], banks[3]]
    for b in range(2):
        kb.dma("sp", kt[:, :], KT[b], W=[kt])
        kb.dma("sp", vt[:, :, :], V[b].rearrange("(n p) e -> p n e", p=128), W=[vt])
        qtiles = [(q0, 512, 0, NKB) for q0 in range(0, S, 512)]
        if ctx_out:
            qtiles.append((S, CTX, S // 128, NKB))
        for (q0, TQ, kb0, kb1) in qtiles:
            qt = qring.next()
            kb.dma("sp", qt[:, :TQ], QT[b][:, q0:q0 + TQ], W=[qt])
            nkb = kb1 - kb0

            def emit_S(i):
                for m in range(2):
                    Sb = banks[4 + (2 * i + m) % 4]
                    blk = kb0 + i
                    kb.op("pe", lambda e, Sb=Sb, m=m, blk=blk: e.matmul(
                        Sb[:, :TQ], kt[64 * m:64 * m + 64, blk * 128:(blk + 1) * 128], qt[64 * m:64 * m + 64, :TQ],
                        start=True, stop=True), R=[kt, qt], W=[Sb])
            emit_S(0)
            for i in range(nkb):
                if i + 1 < nkb:
                    emit_S(i + 1)
                half = i % 2
                Sb0, Sb1 = banks[4 + 2 * half], banks[5 + 2 * half]
                Sv = kb.psum_all[:, (4 + 2 * half) * 512:(6 + 2 * half) * 512].rearrange("p (m q) -> p m q", m=2)
                P = pring.next()
                Pv = P.t[:, :].rearrange("p (m q) -> p m q", m=2)
                kb.op("act", lambda e, Sv=Sv, Pv=Pv: e.activation(out=Pv[:, :, :TQ], in_=Sv[:, :, :TQ], func=AF.Exp, scale=0.125),
                      R=[Sb0, Sb1], W=[P])
                for m in range(2):
                    kb.op("pe", lambda e, m=m, P=P, i=i: e.matmul(
                        O[m][:, :TQ], vt[:, kb0 + i, :], P[:, m * 512:m * 512 + TQ], start=(i == 0), stop=(i == nkb - 1)),
                        R=[P, vt], W=[O[m]])
                if i == 0:
                    kb.op("dve", lambda e, Pv=Pv: e.tensor_copy(out=accv[:, :, :TQ], in_=Pv[:, :, :TQ]), R=[P], W=[R_[0], R_[1]])
                else:
                    kb.op("dve", lambda e, Pv=Pv: e.tensor_tensor(out=accv[:, :, :TQ], in0=accv[:, :, :TQ], in1=Pv[:, :, :TQ], op=ALU.add),
                          R=[P, R_[0], R_[1]], W=[R_[0], R_[1]])
            Rb = [banks[5], banks[6]]
            for m in range(2):
                accs = fr.next()
                kb.op("act", lambda e, accs=accs, m=m: e.activation(out=accs[:, :TQ], in_=R_[m][:, :TQ], func=AF.Copy), R=[R_[m]], W=[accs])
                kb.op("pe", lambda e, accs=accs, m=m: e.matmul(Rb[m][:, :TQ], cst.ones32[:, :], accs[:, :TQ], start=True, stop=True),
                      R=[accs, cst.ones32], W=[Rb[m]])
            R0b = Rb[0]
            r0 = fr.next()
            kb.op("dve", lambda e: e.reciprocal(out=r0[:, :TQ], in_=R0b[:, :TQ]), R=[R0b], W=[r0])
            r1 = fr.next()
            kb.op("dve", lambda e: e.reciprocal(out=r1[:, :TQ], in_=Rb[1][:, :TQ]), R=[Rb[1]], W=[r1])
            kb.op("dve", lambda e: e.tensor_scalar(out=r1[:, :TQ], in0=r1[:, :TQ], scalar1=nlam[:, 0:1], scalar2=None, op0=ALU.mult),
                  R=[r1, nlam], W=[r1])
            o0 = fr.next()
            kb.op("dve", lambda e: e.tensor_tensor(out=o0[:, :TQ], in0=O[0][:, :TQ], in1=r0[:, :TQ], op=ALU.mult),
                  R=[O[0], r0], W=[o0])
            o1 = fr.next()
            kb.op("dve", lambda e: e.tensor_tensor(out=o1[:, :TQ], in0=O[1][:, :TQ], in1=r1[:, :TQ], op=ALU.mult),
                  R=[O[1], r1], W=[o1])
            kb.op("dve", lambda e: e.tensor_tensor(out=o0[:, :TQ], in0=o0[:, :TQ], in1=o1[:, :TQ], op=ALU.add),
                  R=[o0, o1], W=[o0])
            sq = fr.next()
            kb.op("act", lambda e: e.activation(out=sq[:, :TQ], in_=o0[:, :TQ], func=AF.Square), R=[o0], W=[sq])
            Sb = banks[4]
            kb.op("pe", lambda e: e.matmul(Sb[:, :TQ], cst.ones32[:, :], sq[:, :TQ], start=True, stop=True),
                  R=[sq, cst.ones32], W=[Sb])
            rs = fr.next()
            kb.op("act", lambda e: e.activation(out=rs[:, :TQ], in_=Sb[:, :TQ], func=AF.Sqrt, scale=1.0 / 128, bias=eps[:, 0:1]),
                  R=[Sb, eps], W=[rs])
            kb.op("dve", lambda e: e.reciprocal(out=rs[:, :TQ], in_=rs[:, :TQ]), R=[rs], W=[rs])
            ob = fr.next()
            kb.op("dve", lambda e: e.scalar_tensor_tensor(out=ob[:, :TQ], in0=o0[:, :TQ], scalar=gsc[:, 0:1], in1=rs[:, :TQ],
                                                          op0=ALU.mult, op1=ALU.mult), R=[o0, gsc, rs], W=[ob])
            kb.dma("sp", OT[b][:, q0:q0 + TQ], ob[:, :TQ], R=[ob], is_output=True)
    return kb.finish()


def _decl_layer(kb, L):
    return dict(
        wmod=kb.dram(f"wmod{L}", [72, 128, KC, 128], F32),
        bmod=kb.dram(f"bmod{L}", [128, 72], F32),
        normg=kb.dram(f"normg{L}", [128, 3, KC], F32),
    )


def _decl_ffn(kb, L, k):
    return (kb.dram(f"win{L}_{k}", [2 * FC, 128, KC, 128], F32), kb.dram(f"wout{L}_{k}", [KC, 128, FC, 128], F32))


def build_tok(kind, S):
    TL = S // 4
    NT = TL + 64
    kb = KB()
    xT = kb.dram("xT", [D, NT], F32)
    cT = kb.dram("cT", [128, KC, 2], F32)
    xv = xT.rearrange("(kc p) t -> p kc t", p=128)
    layers = {"A": [0], "C": [0, 1], "E1": [1, 2], "E2": [2, 3], "F": [3]}[kind]
    ld = {L: _decl_layer(kb, L) for L in layers}
    cst = emit_consts(kb)
    tabs = {L: emit_mod(kb, cst, cT, ld[L]["wmod"], ld[L]["bmod"], ld[L]["normg"], f"m{L}") for L in layers}
    tb = TokBufs(kb)
    tiles = [(t0, 512, 0) for t0 in range(0, TL, 512)] + [(TL, 64, 1)]
    import os as _os
    if _os.environ.get("TILES_MODE") == "lat":
        tiles = tiles[:-1]
    if _os.environ.get("TILES_MODE") == "ctx":
        tiles = tiles[-1:]
    if kind != "F":
        xo = kb.dram("xo", [D, NT], F32, kind="ExternalOutput")
        xov = xo.rearrange("(kc p) t -> p kc t", p=128)

    def load_x(t0, T):
        kb.dma("sp", tb.xt[:, :, :T], xv[:, :, t0:t0 + T], W=tb.x)

    def store_x(t0, T):
        kb.dma("sp", xov[:, :, t0:t0 + T], tb.xt[:, :, :T], R=tb.x, is_output=True)

    def small(name, shape):
        d = kb.dram(name, shape, F32)
        b = kb.sb(shape, F32, name + "_sb")
        kb.dma("sp", b.t[tuple(slice(None) for _ in shape)], d, W=[b])
        return b

    if kind in ("A", "E2"):
        La = 0 if kind == "A" else 3
        wqk = kb.dram("wqk", [32, 128, KC, 128], F32)
        wv = kb.dram("wv", [128, KC, D], F32)
        ctab = kb.dram("ctab", [128, NT], F32)
        stab = kb.dram("stab", [128, NT], F32)
        QT = kb.dram("QT", [8, 128, NT], BF16, kind="ExternalOutput")
        KT = kb.dram("KT", [8, 128, NT], BF16, kind="ExternalOutput")
        Vd = kb.dram("V", [NT, D], BF16, kind="ExternalOutput")
        qb = QkvBufs(kb, wv)

    if kind == "A":
        f0 = _decl_ffn(kb, 0, 0)
        for (t0, T, r) in tiles:
            load_x(t0, T)
            emit_ffn(kb, cst, tb, T, tabs[0], 0, r, *f0)
            store_x(t0, T)
            emit_norm_mod(kb, cst, tb, T, tabs[0]["A"][1], tabs[0]["Bv"][1], r)
            emit_qkv(kb, tb, qb, T, t0, wqk, ctab, stab, QT, KT, Vd)

    elif kind == "C":
        oT = kb.dram("oT", [D, NT], F32)
        wo = kb.dram("wo", [KC, 128, KC, 128], F32)
        f02 = _decl_ffn(kb, 0, 1)
        f10 = _decl_ffn(kb, 1, 0)
        whin = kb.dram("whin", [24, 128, KC, 128], F32)
        bhin = small("bhin", [128, 24])
        uT = kb.dram("uT", [3 * D, NT], F32, kind="ExternalOutput")
        uo = kb.ring(3, [128, 512], F32, "uo")
        for (t0, T, r) in tiles:
            load_x(t0, T)
            emit_load_h(kb, tb, oT, t0, T)
            emit_post_proj(kb, tb, T, wo, tabs[0]["G"][1], r)
            emit_ffn(kb, cst, tb, T, tabs[0], 2, r, *f02)
            emit_ffn(kb, cst, tb, T, tabs[1], 0, r, *f10)
            store_x(t0, T)
            emit_norm_mod(kb, cst, tb, T, tabs[1]["A"][1], tabs[1]["Bv"][1], r)

            def epi(j0, pss, t0=t0, T=T):
                for gi, ps in enumerate(pss):
                    j = j0 + gi
                    ub = uo.next()
                    kb.op("act", lambda e, ps=ps, ub=ub, j=j: e.activation(
                        out=ub[:, :T], in_=ps[:, :T], func=AF.Identity, bias=bhin[:, j:j + 1]), R=[ps, bhin], W=[ub])
                    kb.dma("sp", uT[j * 128:(j + 1) * 128, t0:t0 + T], ub[:, :T], R=[ub], is_output=True)
            emit_proj(kb, tb, T, whin, 24, epi, group=2)

    elif kind == "E1":
        zT = kb.dram("zT", [D, NT], F32)
        who = kb.dram("who", [KC, 128, KC, 128], F32)
        bho = small("bho", [128, KC])
        f12 = _decl_ffn(kb, 1, 1)
        f20 = _decl_ffn(kb, 2, 0)
        wpw1 = kb.dram("wpw1", [16, 128, KC, 128], F32)
        bpw1 = small("bpw1", [128, 16])
        gT = kb.dram("gT", [D, NT], F32, kind="ExternalOutput")
        uo = kb.ring(3, [128, 512], F32, "uo")
        GB = emit_gb(kb, tabs[1]["G"][1], bho, "GBho")
        for (t0, T, r) in tiles:
            load_x(t0, T)
            emit_load_h(kb, tb, zT, t0, T)
            emit_post_proj(kb, tb, T, who, tabs[1]["G"][1], r, GB)
            emit_ffn(kb, cst, tb, T, tabs[1], 2, r, *f12)
            emit_ffn(kb, cst, tb, T, tabs[2], 0, r, *f20)
            store_x(t0, T)
            emit_norm_mod(kb, cst, tb, T, tabs[2]["A"][1], tabs[2]["Bv"][1], r)

            def epi(j0, pss, t0=t0, T=T):
                for gi in range(0, len(pss), 2):
                    j = (j0 + gi) // 2
                    pa, pg = pss[gi], pss[gi + 1]
                    sgm = tb.tmp.next()
                    kb.op("act", lambda e, pg=pg, sgm=sgm, j=j: e.activation(
                        out=sgm[:, :T], in_=pg[:, :T], func=AF.Sigmoid, bias=bpw1[:, 2 * j + 1:2 * j + 2]),
                        R=[pg, bpw1], W=[sgm])
                    ub = uo.next()
                    kb.op("dve", lambda e, pa=pa, sgm=sgm, ub=ub, j=j: e.scalar_tensor_tensor(
                        out=ub[:, :T], in0=pa[:, :T], scalar=bpw1[:, 2 * j:2 * j + 1], in1=sgm[:, :T],
                        op0=ALU.add, op1=ALU.mult), R=[pa, sgm, bpw1], W=[ub])
                    kb.dma("sp", gT[j * 128:(j + 1) * 128, t0:t0 + T], ub[:, :T], R=[ub], is_output=True)
            emit_proj(kb, tb, T, wpw1, 16, epi, group=2)

    elif kind == "E2":
        NH = TL + 30 + 64 + 30
        gH = kb.dram("gH", [D, NH], F32)
        ghv = gH.rearrange("(kc p) t -> p kc t", p=128)
        wdw = small("wdw", [128, KC, 31])
        bdw = small("bdw", [128, KC])
        lng = small("lng", [128, KC])
        lnb = small("lnb", [128, KC])
        wpw2 = kb.dram("wpw2", [KC, 128, KC, 128], F32)
        bpw2 = small("bpw2", [128, KC])
        f22 = _decl_ffn(kb, 2, 1)
        f30 = _decl_ffn(kb, 3, 0)
        GB = emit_gb(kb, tabs[2]["G"][1], bpw2, "GBpw2")
        lneps = kb.sb([128, 1], F32, "lneps")
        kb.op("dve", lambda e: e.memset(lneps[:, :], 1e-5), W=[lneps])
        gt_t = kb.sbt([128, KC, 512 + 30], F32, "gt")
        gt = [Buf(gt_t, f"gt{k}") for k in range(KC)]
        cu_t = kb.sbt([128, KC, 512], F32, "cu")
        cu = [Buf(cu_t, f"cu{k}") for k in range(KC)]
        mt = kb.sb([128, 512], F32, "lnmean")
        msq = kb.sb([128, 512], F32, "lnmsq")
        for (t0, T, r) in tiles:
            load_x(t0, T)
            c0 = t0 if r == 0 else TL + 30
            kb.dma("sp", gt_t[:, :, :T + 30], ghv[:, :, c0:c0 + T + 30], W=gt)
            ps1 = kb.psum()
            ps2 = kb.psum()
            for kc in range(KC):
                kb.op("dve", lambda e, kc=kc: e.tensor_scalar(
                    out=cu_t[:, kc, :T], in0=gt_t[:, kc, 0:T], scalar1=wdw[:, kc, 0:1], scalar2=bdw[:, kc:kc + 1],
                    op0=ALU.mult, op1=ALU.add), R=[gt[kc], wdw, bdw], W=[cu[kc]])
                for k in range(1, 31):
                    kb.op("dve", lambda e, kc=kc, k=k: e.scalar_tensor_tensor(
                        out=cu_t[:, kc, :T], in0=gt_t[:, kc, k:k + T], scalar=wdw[:, kc, k:k + 1], in1=cu_t[:, kc, :T],
                        op0=ALU.mult, op1=ALU.add), R=[gt[kc], wdw, cu[kc]], W=[cu[kc]])
                sq = tb.sq.next()
                kb.op("act", lambda e, sq=sq, kc=kc: e.activation(out=sq[:, :T], in_=cu_t[:, kc, :T], func=AF.Square),
                      R=[cu[kc]], W=[sq])
                kb.op("pe", lambda e, kc=kc: e.matmul(ps1[:, :T], cst.ones32[:, :], cu_t[:, kc, :T],
                                                      start=(kc == 0), stop=(kc == KC - 1)), R=[cu[kc], cst.ones32], W=[ps1])
                kb.op("pe", lambda e, sq=sq, kc=kc: e.matmul(ps2[:, :T], cst.ones32[:, :], sq[:, :T],
                                                             start=(kc == 0), stop=(kc == KC - 1)), R=[sq, cst.ones32], W=[ps2])
            kb.op("act", lambda e: e.activation(out=mt[:, :T], in_=ps1[:, :T], func=AF.Copy, scale=1.0 / D), R=[ps1], W=[mt])
            kb.op("dve", lambda e: e.tensor_tensor(out=msq[:, :T], in0=mt[:, :T], in1=mt[:, :T], op=ALU.mult), R=[mt], W=[msq])
            kb.op("dve", lambda e: e.scalar_tensor_tensor(out=msq[:, :T], in0=ps2[:, :T], scalar=1.0 / D, in1=msq[:, :T],
                                                          op0=ALU.mult, op1=ALU.subtract), R=[ps2, msq], W=[msq])
            kb.op("act", lambda e: e.activation(out=tb.rstd[:, :T], in_=msq[:, :T], func=AF.Sqrt, bias=lneps[:, 0:1]),
                  R=[msq, lneps], W=[tb.rstd])
            kb.op("dve", lambda e: e.reciprocal(out=tb.rstd[:, :T], in_=tb.rstd[:, :T]), R=[tb.rstd], W=[tb.rstd])
            for kc in range(KC):
                d1 = tb.tmp.next()
                kb.op("dve", lambda e, kc=kc, d1=d1: e.tensor_tensor(out=d1[:, :T], in0=cu_t[:, kc, :T], in1=mt[:, :T], op=ALU.subtract),
                      R=[cu[kc], mt], W=[d1])
                kb.op("dve", lambda e, d1=d1: e.tensor_tensor(out=d1[:, :T], in0=d1[:, :T], in1=tb.rstd[:, :T], op=ALU.mult),
                      R=[d1, tb.rstd], W=[d1])
                kb.op("act", lambda e, kc=kc, d1=d1: e.activation(out=tb.ht[:, kc, :T], in_=d1[:, :T], func=AF.Silu,
                                                                  scale=lng[:, kc:kc + 1], bias=lnb[:, kc:kc + 1]),
                      R=[d1, lng, lnb], W=[tb.h[kc]])
            emit_post_proj(kb, tb, T, wpw2, tabs[2]["G"][1], r, GB)
            emit_ffn(kb, cst, tb, T, tabs[2], 2, r, *f22)
            emit_ffn(kb, cst, tb, T, tabs[3], 0, r, *f30)
            store_x(t0, T)
            emit_norm_mod(kb, cst, tb, T, tabs[3]["A"][1], tabs[3]["Bv"][1], r)
            emit_qkv(kb, tb, qb, T, t0, wqk, ctab, stab, QT, KT, Vd)

    elif kind == "F":
        oT = kb.dram("oT", [D, NT], F32)
        wo = kb.dram("wo", [KC, 128, KC, 128], F32)
        f32_ = _decl_ffn(kb, 3, 1)
        fg = small("fg", [128, KC])
        yT = kb.dram("yT", [D, TL], F32, kind="ExternalOutput")
        yv = yT.rearrange("(kc p) t -> p kc t", p=128)
        for (t0, T, r) in tiles:
            if r == 1:
                continue
            load_x(t0, T)
            emit_load_h(kb, tb, oT, t0, T)
            emit_post_proj(kb, tb, T, wo, tabs[3]["G"][1], r)
            emit_ffn(kb, cst, tb, T, tabs[3], 2, r, *f32_)
            emit_final_norm(kb, cst, tb, T, fg, yv, t0)
    return kb.finish()


TWO_PI = 2.0 * math.pi


def build_hyena(S):
    kb = KB()
    nc = kb.nc
    cst = emit_consts(kb)
    up_l = kb.dram("up_l", [2, 3, 128, S], F32)
    up_c = kb.dram("up_c", [2, 3, 128, CTX], F32)
    fw1 = kb.dram("fw1", [33, 64], F32)
    fw2 = kb.dram("fw2", [64, 64], F32)
    fw3 = kb.dram("fw3", [64, 64], F32)
    fw4 = kb.dram("fw4", [64, 2, 128], F32)
    fbf = kb.dram("fbf", [64, 4], F32)
    wsh = kb.dram("wsh", [128, 3, 3], F32)
    bsh = kb.dram("bsh", [128, 3], F32)
    skp = kb.dram("skp", [128, 1], F32)
    dlt = kb.dram("dlt", [128, 1], F32)
    ident = kb.dram("ident", [128, 128], F32)
    antiid = kb.dram("antiid", [128, 128], F32)
    emb = {S: kb.dram("emb_l", [33, 2 * S], F32), CTX: kb.dram("emb_c", [33, 2 * CTX], F32)}
    tex = {S: kb.dram("tex_l", [1, 2 * S], F32), CTX: kb.dram("tex_c", [1, 2 * CTX], F32)}
    KK = {S: kb.dram("KK_l", [128, 2 * S], BF16, kind="ExternalOutput"),
          CTX: kb.dram("KK_c", [128, 2 * CTX], BF16, kind="ExternalOutput")}
    zo = {S: kb.dram("z_l", [2, 128, S], F32, kind="ExternalOutput"),
          CTX: kb.dram("z_c", [2, 128, CTX], F32, kind="ExternalOutput")}
    up = {S: up_l, CTX: up_c}

    def small(d, shape, name):
        b = kb.sb(shape, F32, name)
        kb.dma("sp", b.t[tuple(slice(None) for _ in shape)], d, W=[b])
        return b
    w1 = small(fw1, [33, 64], "w1s")
    w2 = small(fw2, [64, 64], "w2s")
    w3 = small(fw3, [64, 64], "w3s")
    w4 = small(fw4, [64, 2, 128], "w4s")
    bf_ = small(fbf, [64, 4], "bfs")
    ws = small(wsh, [128, 3, 3], "wss")
    bs = small(bsh, [128, 3], "bss")
    sk = small(skp, [128, 1], "sks")
    dl = small(dlt, [128, 1], "dls")
    idf = small(ident, [128, 128], "idf")
    idb = kb.sb([128, 128], BF16, "idb")
    kb.op("dve", lambda e: e.tensor_copy(out=idb[:, :], in_=idf[:, :]), R=[idf], W=[idb])
    jf = small(antiid, [128, 128], "jf")
    jb = kb.sb([128, 128], BF16, "jb")
    kb.op("dve", lambda e: e.tensor_copy(out=jb[:, :], in_=jf[:, :]), R=[jf], W=[jb])
    xsr = kb.ring(2, [128, 128], BF16, "xsr")
    ndl = kb.sb([128, 1], F32, "ndl")
    kb.op("dve", lambda e: e.tensor_scalar(out=ndl[:, :], in0=dl[:, :], scalar1=-1.0, scalar2=None, op0=ALU.mult), R=[dl], W=[ndl])
    scl = kb.sb([64, 1], F32, "scl")
    kb.op("dve", lambda e: e.tensor_scalar(out=scl[:, :], in0=bf_[:, 3:4], scalar1=1.0 / TWO_PI, scalar2=None, op0=ALU.mult),
          R=[bf_], W=[scl])
    bia = kb.sb([64, 3], F32, "bia")
    kb.op("dve", lambda e: e.tensor_scalar(out=bia[:, :], in0=bf_[:, 0:3], scalar1=scl[:, 0:1], scalar2=8.0,
                                           op0=ALU.mult, op1=ALU.add), R=[bf_, scl], W=[bia])

    embt = kb.ring(2, [33, 512], F32, "embt")
    text = kb.ring(2, [128, 512], F32, "text")
    hd = kb.ring(4, [64, 512], F32, "hd")
    hdi = kb.ring(2, [64, 512], mybir.dt.int32, "hdi")
    hdf = kb.ring(2, [64, 512], F32, "hdf")
    dec = kb.ring(2, [128, 512], F32, "dec")
    kko = kb.ring(3, [128, 512], BF16, "kko")
    kkbuf = Buf(None, "kkdram")
    def gen_filter(L):
        n2 = 2 * L
        for m0 in range(0, n2, 512):
            W_ = min(512, n2 - m0)
            et = embt.next()
            kb.dma("sp", et[:, :W_], emb[L][:, m0:m0 + W_], W=[et])
            tt = text.next()
            kb.dma("sp", tt[:, :W_], bass.AP(tex[L].tensor, m0, [[0, 128], [1, W_]]), W=[tt])
            cur, curK = et, 33
            for li, wl in enumerate((w1, w2, w3)):
                ps = kb.psum()
                kb.op("pe", lambda e, ps=ps, wl=wl, cur=cur, curK=curK: e.matmul(
                    ps[:64, :W_], wl[:curK, :], cur[:curK, :W_], start=True, stop=True), R=[wl, cur], W=[ps])
                u = hd.next()
                kb.op("act", lambda e, ps=ps, u=u, li=li: e.activation(
                    out=u[:, :W_], in_=ps[:64, :W_], func=AF.Identity, scale=scl[:, 0:1], bias=bia[:, li:li + 1]),
                    R=[ps, scl, bia], W=[u])
                ki = hdi.next()
                kb.op("dve", lambda e, u=u, ki=ki: e.tensor_copy(out=ki[:, :W_], in_=u[:, :W_]), R=[u], W=[ki])
                kf = hdf.next()
                kb.op("dve", lambda e, kf=kf, ki=ki: e.tensor_copy(out=kf[:, :W_], in_=ki[:, :W_]), R=[ki], W=[kf])
                kb.op("dve", lambda e, u=u, kf=kf: e.tensor_tensor(out=u[:, :W_], in0=u[:, :W_], in1=kf[:, :W_], op=ALU.subtract),
                      R=[u, kf], W=[u])
                kb.op("dve", lambda e, u=u, kf=kf: e.tensor_scalar(out=kf[:, :W_], in0=u[:, :W_], scalar1=0.5, scalar2=None, op0=ALU.is_ge),
                      R=[u], W=[kf])
                kb.op("dve", lambda e, u=u, kf=kf: e.tensor_tensor(out=u[:, :W_], in0=u[:, :W_], in1=kf[:, :W_], op=ALU.subtract),
                      R=[u, kf], W=[u])
                kb.op("act", lambda e, u=u: e.activation(out=u[:, :W_], in_=u[:, :W_], func=AF.Sin, scale=TWO_PI), R=[u], W=[u])
                cur, curK = u, 64
            dc = dec.next()
            kb.op("act", lambda e, dc=dc, tt=tt: e.activation(out=dc[:, :W_], in_=tt[:, :W_], func=AF.Exp, scale=ndl[:, 0:1]),
                  R=[tt, ndl], W=[dc])
            ko = kko.next()
            segs = []
            if m0 < L:
                segs.append((0, min(W_, L - m0), 1))
            if m0 + W_ > L:
                s0 = max(0, L - m0)
                segs.append((s0, W_, 0))
            for (a0, a1, dirn) in segs:
                ps = kb.psum()
                kb.op("pe", lambda e, ps=ps, cur=cur, a0=a0, a1=a1, dirn=dirn: e.matmul(
                    ps[:, a0:a1], w4[:, dirn, :], cur[:, a0:a1], start=True, stop=True), R=[w4, cur], W=[ps])
                kb.op("dve", lambda e, ps=ps, a0=a0, a1=a1: e.tensor_tensor(
                    out=ko[:, a0:a1], in0=ps[:, a0:a1], in1=dc[:, a0:a1], op=ALU.mult), R=[ps, dc, ko], W=[ko])
            kb.dma("sp", KK[L][:, m0:m0 + W_], ko[:, :W_], R=[ko, kkbuf], is_output=True)
            yield
    def kk_wait():
        for tk in kkbuf.readers:
            kb._wait("sp", tk)

    TC = 512
    TG = 16
    U = kb.ring(2, [128, 3, TC + 2], F32, "U")
    cvt = [kb.ring(2, [128, TC], F32, f"cv{p}") for p in range(3)]
    vvb = kb.ring(2, [128, TC], BF16, "vvb")
    zt = kb.ring(2, [128, TC], F32, "zt")

    def load_conv(L, b, c0, W_, parts):
        u = U.next()
        lo = max(c0 - 1, 0)
        hi = min(c0 + W_ + 1, L)
        if c0 == 0:
            kb.op("dve", lambda e: e.memset(u[:, :, 0:1], 0.0), W=[u])
        if c0 + W_ == L:
            kb.op("dve", lambda e: e.memset(u[:, :, W_ + 1:W_ + 2], 0.0), W=[u])
        kb.dma("sp", u[:, :, lo - (c0 - 1):hi - (c0 - 1)], up[L][b][:, :, lo:hi].rearrange("q c t -> c q t"), R=[u], W=[u])
        out = {}
        for p in parts:
            cv = cvt[p].next()
            kb.op("dve", lambda e, p=p, cv=cv: e.tensor_scalar(
                out=cv[:, :W_], in0=u[:, p, 0:W_], scalar1=ws[:, p, 0:1], scalar2=bs[:, p:p + 1],
                op0=ALU.mult, op1=ALU.add), R=[u, ws, bs], W=[cv])
            for k in (1, 2):
                kb.op("dve", lambda e, p=p, cv=cv, k=k: e.scalar_tensor_tensor(
                    out=cv[:, :W_], in0=u[:, p, k:k + W_], scalar=ws[:, p, k:k + 1], in1=cv[:, :W_],
                    op0=ALU.mult, op1=ALU.add), R=[u, ws, cv], W=[cv])
            out[p] = cv
        return out

    NBmax = S // 128
    vvT = kb.sb([128, 2 * NBmax, 128], BF16, "vvT")
    yT = kb.sb([128, 2 * NBmax, 128], BF16, "yT")
    tg = kb.ring(3, [128, TG * 128], BF16, "tg")

    def gen_p1(L):
        nb = L // 128
        tcw = min(TC, L)
        for b in range(2):
            for c0 in range(0, L, tcw):
                cv = load_conv(L, b, c0, tcw, (1, 2))
                vb = vvb.next()
                kb.op("dve", lambda e, cv=cv, vb=vb: e.tensor_tensor(out=vb[:, :tcw], in0=cv[1][:, :tcw], in1=cv[2][:, :tcw], op=ALU.mult),
                      R=[cv[1], cv[2]], W=[vb])
                for bi in range(tcw // 128):
                    blk = c0 // 128 + bi
                    ps = kb.psum()
                    pv = ps.t[:, 0:64].bitcast(BF16)
                    kb.op("pe", lambda e, pv=pv, vb=vb, bi=bi: e.transpose(pv, vb[:, bi * 128:(bi + 1) * 128], idb[:, :]),
                          R=[vb, idb], W=[ps])
                    xs = xsr.next()
                    kb.op("act", lambda e, pv=pv, xs=xs: e.activation(out=xs[:, :], in_=pv, func=AF.Copy), R=[ps], W=[xs])
                    ps2 = kb.psum()
                    kb.op("pe", lambda e, ps2=ps2, xs=xs: e.matmul(ps2[:, 0:128], jb[:, :], xs[:, :], start=True, stop=True),
                          R=[jb, xs], W=[ps2])
                    kb.op("act", lambda e, ps2=ps2, b=b, blk=blk: e.activation(out=vvT[:, 2 * blk + b, :], in_=ps2[:, 0:128], func=AF.Copy),
                          R=[ps2], W=[vvT])
                yield

    def phase23(L):
        nb = L // 128
        tcw = min(TC, L)
        ND = 2 * nb - 1
        order = [nb - 1] + [d for d in range(ND) if d != nb - 1]
        for c in range(128):
            ps = kb.psum()
            psv = ps.t[:, 0:2 * nb]
            g0s = list(range(0, ND, TG))
            g0s.sort(key=lambda g0: 0 if g0 <= nb - 1 < g0 + TG else 1)
            for gi_, g0 in enumerate(g0s):
                g = min(TG, ND - g0)
                t = tg.next()
                d0 = g0 - (nb - 1)
                src = bass.AP(KK[L].tensor, c * 2 * L + L + 128 * d0 - 127, [[1, 128], [1, g * 128]])
                kb.dma("sp", t[:, :g * 128], src, W=[t])
                dis = [d for d in order if g0 <= d < g0 + g]
                lastg = (gi_ == len(g0s) - 1)

                def mm(e, t=t, g0=g0, dis=dis, lastg=lastg):
                    i = None
                    for n_, di in enumerate(dis):
                        dlt_ = di - (nb - 1)
                        b0, b1 = max(0, -dlt_), min(nb, nb - dlt_)
                        i = e.matmul(psv[:, 2 * (b0 + dlt_):2 * (b1 + dlt_)], t[:, (di - g0) * 128:(di - g0 + 1) * 128],
                                     vvT[:, 2 * b0:2 * b1, c], start=(dlt_ == 0), stop=(lastg and n_ == len(dis) - 1),
                                     skip_group_check=True)
                    return i
                kb.op("pe", mm, R=[t, vvT], W=[ps])
            kb.op("act", lambda e, psv=psv, c=c: e.activation(out=yT[:, 0:2 * nb, c], in_=psv, func=AF.Copy), R=[ps], W=[yT])
        for b in range(2):
            for c0 in range(0, L, tcw):
                cv = load_conv(L, b, c0, tcw, (0, 1, 2))
                vv = cv[1]
                kb.op("dve", lambda e, cv=cv: e.tensor_tensor(out=cv[1][:, :tcw], in0=cv[1][:, :tcw], in1=cv[2][:, :tcw], op=ALU.mult),
                      R=[cv[1], cv[2]], W=[cv[1]])
                ps = kb.psum()
                pv = ps.t[:, 0:512].bitcast(BF16)
                for bi in range(tcw // 128):
                    blk = c0 // 128 + bi
                    kb.op("pe", lambda e, pv=pv, bi=bi, b=b, blk=blk: e.transpose(
                        pv[:, bi * 128:(bi + 1) * 128], yT[:, 2 * blk + b, :], idb[:, :]), R=[yT, idb], W=[ps])
                z = zt.next()
                kb.op("dve", lambda e, pv=pv, vv=vv, z=z: e.scalar_tensor_tensor(
                    out=z[:, :tcw], in0=vv[:, :tcw], scalar=sk[:, 0:1], in1=pv[:, :tcw], op0=ALU.mult, op1=ALU.add),
                    R=[vv, sk, ps], W=[z])
                kb.op("dve", lambda e, z=z, cv=cv: e.tensor_tensor(out=z[:, :tcw], in0=z[:, :tcw], in1=cv[0][:, :tcw], op=ALU.mult),
                      R=[z, cv[0]], W=[z])
                kb.dma("sp", zo[L][b][:, c0:c0 + tcw], z[:, :tcw], R=[z], is_output=True)
    def drain(*gens):
        gens = [g for g in gens]
        while gens:
            for g in list(gens):
                try:
                    next(g)
                except StopIteration:
                    gens.remove(g)
    drain(gen_filter(S), gen_p1(S))
    drain(gen_filter(CTX))
    kk_wait()
    phase23(S)
    drain(gen_p1(CTX))
    phase23(CTX)
    return kb.finish()


def _rope_tables(S):
    n = np.arange(S)
    row = (n // GRID_W).astype(np.float32)
    col = (n % GRID_W).astype(np.float32)
    inv = (np.float32(10000.0) ** (-(2.0 * np.arange(16, dtype=np.float32)) / 32)).astype(np.float32)
    C = np.zeros((64, S), np.float32)
    Sg = np.zeros((64, S), np.float32)
    for d in range(64):
        ang = ((row if d < 32 else col) * inv[d % 16]).astype(np.float32)
        C[d] = np.cos(ang)
        Sg[d] = np.sin(ang) * (-1.0 if (d % 32) < 16 else 1.0)
    return np.concatenate([C, C], 0), np.concatenate([Sg, Sg], 0)


def _hy_tables(L):
    f32 = np.float32
    t = np.linspace(0.0, 1.0, L, dtype=f32)
    wpos = (f32(2.0 * math.pi / L) * np.arange(L, dtype=f32)).astype(f32)
    bands = np.linspace(1e-4, 15.0, 16, dtype=f32)
    fw = wpos[:, None] * bands[None, :]
    emb = np.concatenate([t[:, None], np.cos(fw), -np.sin(fw)], -1).astype(f32)
    m = np.arange(2 * L)
    idx = np.where(m >= L, m - L, L - m)
    idx[0] = 0
    return np.ascontiguousarray(emb[idx].T), np.ascontiguousarray(t[idx][None, :])


def _ffn_w(w_in, w_out):
    order = []
    for j in range(FC):
        order += [j, FC + j]
    return tile_w(w_in, order), tile_w(w_out)


_PROGS = {}


def _prog(key, fn):
    if key not in _PROGS:
        _PROGS[key] = fn()
    return _PROGS[key]


def _launch(nc, in_maps):
    res = run_bass_kernel_spmd(nc, in_maps, core_ids=list(range(NCORE)))
    return res.results


def _forward(S, x, c, ctx, c_ctx, w_mod, b_mod, norm_g, w_ffn_in, w_ffn_out,
             attn_w_qkv, attn_w_o, attn_lambda, attn_subln_g,
             hy_w_in, hy_b_in, hy_w_short, hy_b_short, hy_f_w1, hy_f_b1, hy_f_w2, hy_f_b2,
             hy_f_w3, hy_f_b3, hy_f_freq, hy_f_w4, hy_skip, hy_w_out, hy_b_out,
             cv_w_pw1, cv_b_pw1, cv_w_dw, cv_b_dw, cv_ln_g, cv_ln_b, cv_w_pw2, cv_b_pw2, final_g):
    import ml_dtypes
    f32 = np.float32
    TL = S // 4
    NT = TL + 64
    A = lambda a: np.asarray(a, dtype=f32)
    x, c, ctx, c_ctx = A(x), A(c), A(ctx), A(c_ctx)
    cores = [(r // 4, r % 4) for r in range(NCORE)]

    xT = [np.ascontiguousarray(np.concatenate([x[b, q * TL:(q + 1) * TL], ctx[b, q * 64:(q + 1) * 64]], 0).T) for b, q in cores]
    cT = [np.ascontiguousarray(np.stack([fm_vec(c[b]), fm_vec(c_ctx)], -1)) for b, q in cores]

    def layer_in(L):
        return {f"wmod{L}": tile_w(A(w_mod[L])), f"bmod{L}": fm_vec(A(b_mod[L])), f"normg{L}": fm_vec(A(norm_g[L]))}

    def ffn_in(L, k):
        wi, wo_ = _ffn_w(A(w_ffn_in[L, k]), A(w_ffn_out[L, k]))
        return {f"win{L}_{k}": wi, f"wout{L}_{k}": wo_}

    Ct, St = _rope_tables(S)
    ctab, stab = [], []
    for b, q in cores:
        ctab.append(np.ascontiguousarray(np.concatenate([Ct[:, q * TL:(q + 1) * TL], np.ones((128, 64), f32)], 1)))
        stab.append(np.ascontiguousarray(np.concatenate([St[:, q * TL:(q + 1) * TL], np.zeros((128, 64), f32)], 1)))

    perm = np.arange(D)
    for i in range(D):
        d = i % 64
        perm[i] = i + 16 if (d % 32) < 16 else i - 16

    def qkv_in(j):
        W = A(attn_w_qkv[j])
        Wq, Wk, Wv = W[:, :D], W[:, D:2 * D], W[:, 2 * D:]
        tl = [tile_w(Wq), tile_w(Wq[:, perm]), tile_w(Wk), tile_w(Wk[:, perm])]
        wqk = np.stack(tl, 1).reshape(32, 128, KC, 128)
        wv = np.ascontiguousarray(Wv.reshape(KC, 128, D).transpose(1, 0, 2))
        return {"wqk": np.ascontiguousarray(wqk), "wv": wv}

    def run_attn(res, j, lam_init, ctx_out):
        nc = _prog(("attn", S, j, ctx_out), lambda: build_attn(S, lam_init, ctx_out))
        maps = []
        for hd in range(8):
            QTh = np.stack([np.concatenate([res[b * 4 + q]["QT"][hd][:, :TL] for q in range(4)] +
                                           [res[b * 4 + q]["QT"][hd][:, TL:] for q in range(4)], 1) for b in range(2)])
            KTh = np.stack([np.concatenate([res[b * 4 + q]["KT"][hd][:, :TL] for q in range(4)] +
                                           [res[b * 4 + q]["KT"][hd][:, TL:] for q in range(4)], 1) for b in range(2)])
            Vh = np.stack([np.concatenate([res[b * 4 + q]["V"][:TL, hd * 128:(hd + 1) * 128] for q in range(4)] +
                                          [res[b * 4 + q]["V"][TL:, hd * 128:(hd + 1) * 128] for q in range(4)], 0) for b in range(2)])
            maps.append({"QT": np.ascontiguousarray(QTh), "KT": np.ascontiguousarray(KTh), "V": np.ascontiguousarray(Vh),
                         "lamv": A(attn_lambda[j]), "subg": A(attn_subln_g[j]).reshape(128, 1)})
        ro = _launch(nc, maps)
        oT = []
        for b, q in cores:
            o = np.zeros((D, NT), f32)
            for hd in range(8):
                o[hd * 128:(hd + 1) * 128, :TL] = ro[hd]["OT"][b][:, q * TL:(q + 1) * TL]
                if ctx_out:
                    o[hd * 128:(hd + 1) * 128, TL:] = ro[hd]["OT"][b][:, S + q * 64:S + (q + 1) * 64]
            oT.append(o)
        return oT

    ncA = _prog(("A", S), lambda: build_tok("A", S))
    shared = {}
    shared.update(layer_in(0)); shared.update(ffn_in(0, 0)); shared.update(qkv_in(0))
    resA = _launch(ncA, [dict(shared, xT=xT[r], cT=cT[r], ctab=ctab[r], stab=stab[r]) for r in range(NCORE)])
    xT = [resA[r]["xo"] for r in range(NCORE)]
    oT = run_attn(resA, 0, 0.8 - 0.6 * math.exp(-0.3 * 0), True)

    ncC = _prog(("C", S), lambda: build_tok("C", S))
    shared = {}
    shared.update(layer_in(0)); shared.update(layer_in(1)); shared.update(ffn_in(0, 1)); shared.update(ffn_in(1, 0))
    shared["wo"] = tile_w(A(attn_w_o[0]))
    shared["whin"] = tile_w(A(hy_w_in[0]))
    shared["bhin"] = fm_vec(A(hy_b_in[0]))
    resC = _launch(ncC, [dict(shared, xT=xT[r], cT=cT[r], oT=oT[r]) for r in range(NCORE)])
    xT = [resC[r]["xo"] for r in range(NCORE)]

    ncD = _prog(("D", S), lambda: build_hyena(S))
    emb_l, tex_l = _hy_tables(S)
    emb_c, tex_c = _hy_tables(CTX)
    max_decay = math.log(1e-2) / 0.3
    min_decay = math.log(1e-2) / 1.5
    deltas = np.abs(np.linspace(min_decay, max_decay, D, dtype=f32)).astype(f32)
    U = [np.concatenate([resC[b * 4 + q]["uT"][:, :TL] for q in range(4)], 1).reshape(3, D, S) for b in range(2)]
    Uc = [np.concatenate([resC[b * 4 + q]["uT"][:, TL:] for q in range(4)], 1).reshape(3, D, CTX) for b in range(2)]
    wsh_full = A(hy_w_short[0])
    bsh_full = A(hy_b_short[0])
    w4 = A(hy_f_w4[0])
    maps = []
    for r in range(NCORE):
        cs = slice(r * 128, (r + 1) * 128)
        maps.append({
            "up_l": np.ascontiguousarray(np.stack([U[b][:, cs, :] for b in range(2)])),
            "up_c": np.ascontiguousarray(np.stack([Uc[b][:, cs, :] for b in range(2)])),
            "fw1": A(hy_f_w1[0]), "fw2": A(hy_f_w2[0]), "fw3": A(hy_f_w3[0]),
            "fw4": np.ascontiguousarray(np.stack([w4[:, cs], w4[:, D + r * 128:D + (r + 1) * 128]], 1)),
            "fbf": np.ascontiguousarray(np.stack([A(hy_f_b1[0]), A(hy_f_b2[0]), A(hy_f_b3[0]), A(hy_f_freq[0])], 1)),
            "wsh": np.ascontiguousarray(wsh_full.reshape(3, 3, D)[:, :, cs].transpose(2, 1, 0)),
            "bsh": np.ascontiguousarray(bsh_full.reshape(3, D)[:, cs].T),
            "skp": A(hy_skip[0])[cs].reshape(128, 1), "dlt": deltas[cs].reshape(128, 1),
            "ident": np.eye(128, dtype=f32), "antiid": np.ascontiguousarray(np.eye(128, dtype=f32)[::-1]),
            "emb_l": emb_l, "emb_c": emb_c, "tex_l": tex_l, "tex_c": tex_c,
        })
    resD = _launch(ncD, maps)
    zT = []
    for b, q in cores:
        z = np.zeros((D, NT), f32)
        for r in range(NCORE):
            z[r * 128:(r + 1) * 128, :TL] = resD[r]["z_l"][b][:, q * TL:(q + 1) * TL]
            z[r * 128:(r + 1) * 128, TL:] = resD[r]["z_c"][b][:, q * 64:(q + 1) * 64]
        zT.append(z)

    ncE1 = _prog(("E1", S), lambda: build_tok("E1", S))
    shared = {}
    shared.update(layer_in(1)); shared.update(layer_in(2)); shared.update(ffn_in(1, 1)); shared.update(ffn_in(2, 0))
    shared["who"] = tile_w(A(hy_w_out[0]))
    shared["bho"] = fm_vec(A(hy_b_out[0]))
    po = []
    for j in range(8):
        po += [j, 8 + j]
    shared["wpw1"] = tile_w(A(cv_w_pw1[0]), po)
    shared["bpw1"] = np.ascontiguousarray(fm_vec(A(cv_b_pw1[0]))[:, po])
    resE1 = _launch(ncE1, [dict(shared, xT=xT[r], cT=cT[r], zT=zT[r]) for r in range(NCORE)])
    xT = [resE1[r]["xo"] for r in range(NCORE)]
    gH = []
    Gl = [np.pad(np.concatenate([resE1[b * 4 + q]["gT"][:, :TL] for q in range(4)], 1), ((0, 0), (15, 15))) for b in range(2)]
    Gc = [np.pad(np.concatenate([resE1[b * 4 + q]["gT"][:, TL:] for q in range(4)], 1), ((0, 0), (15, 15))) for b in range(2)]
    for b, q in cores:
        gH.append(np.ascontiguousarray(np.concatenate([Gl[b][:, q * TL:q * TL + TL + 30], Gc[b][:, q * 64:q * 64 + 94]], 1)))

    ncE2 = _prog(("E2", S), lambda: build_tok("E2", S))
    shared = {}
    shared.update(layer_in(2)); shared.update(layer_in(3)); shared.update(ffn_in(2, 1)); shared.update(ffn_in(3, 0))
    shared.update(qkv_in(1))
    shared["wdw"] = np.ascontiguousarray(fm_vec(A(cv_w_dw[0])).transpose(0, 2, 1))
    shared["bdw"] = fm_vec(A(cv_b_dw[0]))
    shared["lng"] = fm_vec(A(cv_ln_g[0]))
    shared["lnb"] = fm_vec(A(cv_ln_b[0]))
    shared["wpw2"] = tile_w(A(cv_w_pw2[0]))
    shared["bpw2"] = fm_vec(A(cv_b_pw2[0]))
    resE2 = _launch(ncE2, [dict(shared, xT=xT[r], cT=cT[r], gH=gH[r], ctab=ctab[r], stab=stab[r]) for r in range(NCORE)])
    xT = [resE2[r]["xo"] for r in range(NCORE)]
    oT = run_attn(resE2, 1, 0.8 - 0.6 * math.exp(-0.3 * 3), False)

    ncF = _prog(("F", S), lambda: build_tok("F", S))
    shared = {}
    shared.update(layer_in(3)); shared.update(ffn_in(3, 1))
    shared["wo"] = tile_w(A(attn_w_o[1]))
    shared["fg"] = fm_vec(A(final_g))
    resF = _launch(ncF, [dict(shared, xT=xT[r], cT=cT[r], oT=oT[r]) for r in range(NCORE)])
    out = np.zeros((B, S, D), f32)
    for r, (b, q) in enumerate(cores):
        out[b, q * TL:(q + 1) * TL, :] = resF[r]["yT"].T
    return out


def kernel(**inputs):
    return _forward(SEQ, **inputs)
```

```python
from contextlib import ExitStack
import math
import numpy as np
import concourse.bass as bass
import concourse.mybir as mybir
from concourse.bass_utils import run_bass_kernel_spmd

F32 = mybir.dt.float32
BF16 = mybir.dt.bfloat16
AF = mybir.ActivationFunctionType
ALU = mybir.AluOpType
NDMA_SEMS = 64
DRAM_SHAPES = {}
NSW_SEMS = 2

D = 1024
KC = 8
B = 2
SEQ = 16384
GRID_W = 64
CTX = 256
DFF = 2816
FC = DFF // 128
NCORE = 8
TOK_CTX = CTX * B // NCORE
EPS = 1e-6


class Buf:
    __slots__ = ("t", "last_w", "readers", "name")

    def __init__(self, t, name=""):
        self.t = t
        self.last_w = None
        self.readers = []
        self.name = name

    def __getitem__(self, idx):
        return self.t[idx]


class Ring:
    def __init__(self, bufs):
        self.bufs = bufs
        self.i = 0

    def next(self):
        b = self.bufs[self.i % len(self.bufs)]
        self.i += 1
        return b


class KB:
    def __init__(self):
        self.nc = bass.Bass("TRN2", target_bir_lowering=False)
        self.es = ExitStack()
        nc = self.nc
        self.eng = {"pe": nc.tensor, "act": nc.scalar, "dve": nc.vector, "pool": nc.gpsimd, "sp": nc.sync}
        self.csem = {}
        self.ccnt = {}
        for e in ("pe", "act", "dve", "pool"):
            self.csem[e] = self.es.enter_context(nc.semaphore("c_" + e))
            self.ccnt[e] = 0
        self.dsem = [self.es.enter_context(nc.semaphore(f"d{i}")) for i in range(NDMA_SEMS)]
        self.dval = [0] * NDMA_SEMS
        self.dtok = [None] * NDMA_SEMS
        self.dnext = 0
        self.wsem = [self.es.enter_context(nc.semaphore(f"w{i}")) for i in range(NSW_SEMS)]
        self.wtok = [None] * NSW_SEMS
        self.wgen = [0] * NSW_SEMS
        self.wnext = 0
        self.cur_sw = []
        self.sw_waiters = {}
        self.waited = {e: {} for e in self.eng}
        self.nbuf = 0
        self.psum_banks = []
        self.psum_i = 0
        self.out_toks = []
        self.cast_i = 0
        for i in range(8):
            t = self.es.enter_context(nc.psum_tensor(f"ps{i}", [128, 512], F32))
            self.psum_banks.append(Buf(t, f"ps{i}"))

    def sbt(self, shape, dt, name=None):
        self.nbuf += 1
        name = "s_" + (name or f"b{self.nbuf}")
        return self.es.enter_context(self.nc.sbuf_tensor(name, list(shape), dt))

    def sb(self, shape, dt, name=None):
        t = self.sbt(shape, dt, name)
        return Buf(t, name or "")

    def ring(self, n, shape, dt, name=None):
        self.nbuf += 1
        name = name or f"r{self.nbuf}"
        return Ring([self.sb(shape, dt, f"{name}_{i}") for i in range(n)])

    @property
    def stg(self):
        if not hasattr(self, "_stg"):
            self._stg = self.ring(3, [128, STG_N], F32, "stg")
        return self._stg

    def psum(self):
        b = self.psum_banks[self.psum_i % 8]
        self.psum_i += 1
        return b

    def dram(self, name, shape, dt, kind="ExternalInput"):
        DRAM_SHAPES.setdefault(id(self.nc), {})[name] = (list(shape), dt, kind)
        return self.nc.dram_tensor(name, list(shape), dt, kind=kind).ap()

    def _wait(self, e, tok):
        if tok is None:
            return
        sem, val, key = tok
        if e == "pe" and key == "c_pe":
            return
        if self.waited[e].get(key, 0) >= val:
            return
        if key[0] == "w":
            k, gen = key[1:].split("_")
            if self.wgen[int(k)] != int(gen):
                raise RuntimeError(f"stale SW-DMA token {key} waited by {e}")
            self.cur_sw.append(key)
        self.eng[e].wait_ge(sem, val)
        self.waited[e][key] = val

    def _deps(self, e, R, W):
        for b in R:
            self._wait(e, b.last_w)
        for b in W:
            self._wait(e, b.last_w)
            for tk in b.readers:
                self._wait(e, tk)

    def _commit(self, tok, R, W):
        for b in R:
            b.readers.append(tok)
            if len(b.readers) > 16:
                d = {}
                for tk in b.readers:
                    if tk[2] not in d or d[tk[2]][1] < tk[1]:
                        d[tk[2]] = tk
                b.readers = list(d.values())
        for b in W:
            b.last_w = tok
            b.readers = []

    def op(self, e, fn, R=(), W=()):
        self._deps(e, R, W)
        inst = fn(self.eng[e])
        self.ccnt[e] += 1
        inst.then_inc(self.csem[e], 1)
        tok = (self.csem[e], self.ccnt[e], "c_" + e)
        self._reg_waiters(e, tok)
        self._commit(tok, R, W)
        return tok

    def _reg_waiters(self, e, tok):
        for key in self.cur_sw:
            self.sw_waiters.setdefault(key, {})[e] = tok
        self.cur_sw = []

    def dma(self, q, out, in_, R=(), W=(), is_output=False, **kw):
        if is_output and q == "sp":
            q = "act"
        if q == "pool":
            k = self.wnext
            self.wnext = (self.wnext + 1) % NSW_SEMS
            if self.wtok[k] is not None:
                self._wait(q, self.wtok[k])
                for we, wt in self.sw_waiters.pop(self.wtok[k][2], {}).items():
                    if we != "pool":
                        self._wait(q, wt)
                self.cur_sw = [x for x in self.cur_sw if x != self.wtok[k][2]]
                self.wgen[k] += 1
                self.eng[q].sem_clear(self.wsem[k])
            self._deps(q, R, W)
            inst = self.eng[q].dma_start(out=out, in_=in_, **kw)
            inst.then_inc(self.wsem[k], 16)
            tok = (self.wsem[k], 16, f"w{k}_{self.wgen[k]}")
            self.wtok[k] = tok
        else:
            k = self.dnext
            self.dnext = (self.dnext + 1) % NDMA_SEMS
            self._wait(q, self.dtok[k])
            self._deps(q, R, W)
            inst = self.eng[q].dma_start(out=out, in_=in_, **kw)
            self.dval[k] += 16
            inst.then_inc(self.dsem[k], 16)
            tok = (self.dsem[k], self.dval[k], f"d{k}")
            self.dtok[k] = tok
        self._reg_waiters(q, tok)
        self._commit(tok, R, W)
        if is_output:
            self.out_toks.append(tok)
        return tok

    def finish(self):
        for tok in self.out_toks:
            self._wait("sp", tok)
        for k in range(NDMA_SEMS):
            self._wait("sp", self.dtok[k])
        for k in range(NSW_SEMS):
            self._wait("sp", self.wtok[k])
        for e in ("pe", "act", "dve", "pool"):
            if self.ccnt[e]:
                self._wait("sp", (self.csem[e], self.ccnt[e], "c_" + e))
        self.es.close()
        return self.nc


def tile_w(W, order=None):
    K, N = W.shape
    kc = K // 128
    J = N // 128
    Wt = W.reshape(kc, 128, J, 128).transpose(2, 1, 0, 3)
    if order is not None:
        Wt = Wt[np.asarray(order)]
    return np.ascontiguousarray(Wt, dtype=np.float32)


def fm_vec(v):
    v = np.asarray(v, dtype=np.float32)
    lead = v.shape[:-1]
    n = v.shape[-1] // 128
    vv = v.reshape(lead + (n, 128))
    vv = np.moveaxis(vv, -1, 0)
    return np.ascontiguousarray(vv)


class Consts:
    pass


STG_N = 3072
CAST_ENGINES = ("act", "dve", "pool", "act", "dve")


def stage_cast(kb, dst_ap, src_ap, pat, W, **dims):
    st = kb.stg.next()
    n = 1
    for d in dst_ap.shape[1:]:
        n *= d
    assert n <= STG_N, n
    view = st.t[:dst_ap.shape[0], 0:n]
    if pat is not None:
        view = view.rearrange(pat, **dims)
    kb.dma("sp", view, src_ap, W=[st])
    eng = CAST_ENGINES[kb.cast_i % len(CAST_ENGINES)]
    kb.cast_i += 1
    if eng == "act":
        kb.op("act", lambda e: e.activation(out=dst_ap, in_=view, func=AF.Copy), R=[st], W=W)
    else:
        kb.op(eng, lambda e: e.tensor_copy(out=dst_ap, in_=view), R=[st], W=W)


def emit_consts(kb):
    c = Consts()
    c.ones32 = kb.sb([128, 128], F32, "ones32")
    kb.op("dve", lambda e: e.memset(c.ones32[:, :], 1.0), W=[c.ones32])
    c.eps = kb.sb([128, 1], F32, "epsc")
    kb.op("dve", lambda e: e.memset(c.eps[:, :], EPS), W=[c.eps])
    c.ones16 = kb.sb([128, 128], BF16, "ones16")
    kb.op("dve", lambda e: e.memset(c.ones16[:, :], 1.0), W=[c.ones16])
    return c


def emit_mod(kb, cst, cT_d, wmod_d, bmod_d, normg_d, name):
    sc = kb.sb([128, KC, 2], F32, name + "_sc")
    kb.dma("sp", sc[:, :, :], cT_d, W=[sc])
    sil = kb.sb([128, KC, 2], F32, name + "_sil")
    kb.op("act", lambda e: e.activation(out=sil[:, :, :], in_=sc[:, :, :], func=AF.Silu), R=[sc], W=[sil])
    bm = kb.sb([128, 72], F32, name + "_bm")
    kb.dma("sp", bm[:, :], bmod_d, W=[bm])
    ng = kb.sb([128, 3, KC], F32, name + "_ng")
    kb.dma("sp", ng[:, :, :], normg_d, W=[ng])
    if not hasattr(kb, "_modring"):
        kb._modring = kb.ring(3, [128, KC, 128], F32, "modw")
    wring = kb._modring
    ps = kb.psum()
    first = True
    for j in range(72):
        wb = wring.next()
        kb.dma("sp", wb[:, :, :], wmod_d[j], W=[wb])

        def mm(e, wb=wb, j=j):
            for kc in range(KC):
                i = e.matmul(ps[:, 2 * j:2 * j + 2], wb[:, kc, :], sil[:, kc, :],
                             start=(kc == 0), stop=(kc == KC - 1))
            return i
        kb.op("pe", mm, R=[wb, sil], W=[ps])
    mod = kb.sb([128, 72, 2], F32, name + "_mod")
    kb.op("dve", lambda e: e.tensor_tensor(
        out=mod[:, :, :], in0=ps[:, 0:144].rearrange("p (j r) -> p j r", r=2),
        in1=bm[:, :, None].broadcast_to([128, 72, 2]), op=ALU.add), R=[ps, bm], W=[mod])
    tabs = {"A": [], "Bv": [], "G": [], "mod": mod}
    for s in range(3):
        A = kb.sb([128, KC, 2], F32, f"{name}_A{s}")
        kb.op("dve", lambda e, A=A, s=s: e.scalar_tensor_tensor(
            out=A[:, :, :], in0=mod[:, (3 * s + 1) * 8:(3 * s + 2) * 8, :], scalar=1.0,
            in1=ng[:, s, :, None].broadcast_to([128, KC, 2]), op0=ALU.add, op1=ALU.mult),
            R=[mod, ng], W=[A])
        G = kb.sb([128, KC, 2], F32, f"{name}_G{s}")
        gs = 1.0 if s == 1 else 0.5
        kb.op("dve", lambda e, G=G, s=s, gs=gs: e.tensor_scalar(
            out=G[:, :, :], in0=mod[:, (3 * s + 2) * 8:(3 * s + 3) * 8, :], scalar1=gs, scalar2=None,
            op0=ALU.mult), R=[mod], W=[G])
        tabs["A"].append(A)
        tabs["G"].append(G)
        tabs["Bv"].append((mod, (3 * s) * 8))
    return tabs


class TokBufs:
    def __init__(self, kb):
        self.xt = kb.sbt([128, KC, 512], F32, "xt")
        self.x = [Buf(self.xt, f"x{k}") for k in range(KC)]
        self.ht = kb.sbt([128, KC, 512], BF16, "ht")
        self.h = [Buf(self.ht, f"h{k}") for k in range(KC)]
        self.at = kb.sbt([128, FC, 512], BF16, "at")
        self.a = [Buf(self.at, f"a{k}") for k in range(FC)]
        self.sq = kb.ring(3, [128, 512], BF16, "sq")
        self.rstd = kb.sb([128, 512], F32, "rstd")
        self.tmp = kb.ring(3, [128, 512], F32, "tmp")
        self.w8 = kb.ring(3, [128, 2, KC, 128], BF16, "w8")
        self.w22 = kb.ring(3, [128, FC, 128], BF16, "w22")


def emit_norm_mod(kb, cst, tb, T, A, Bv, r):
    ps = kb.psum()
    for kc in range(KC):
        sq = tb.sq.next()
        kb.op("act", lambda e, sq=sq, kc=kc: e.activation(out=sq[:, :T], in_=tb.xt[:, kc, :T], func=AF.Square),
              R=[tb.x[kc]], W=[sq])
        kb.op("pe", lambda e, sq=sq, kc=kc: e.matmul(ps[:, :T], cst.ones16[:, :], sq[:, :T],
                                                      start=(kc == 0), stop=(kc == KC - 1)),
              R=[sq, cst.ones16], W=[ps])
    kb.op("act", lambda e: e.activation(out=tb.rstd[:, :T], in_=ps[:, :T], func=AF.Sqrt, scale=1.0 / D, bias=cst.eps[:, 0:1]),
          R=[ps, cst.eps], W=[tb.rstd])
    kb.op("dve", lambda e: e.reciprocal(out=tb.rstd[:, :T], in_=tb.rstd[:, :T]), R=[tb.rstd], W=[tb.rstd])
    modb, base = Bv
    for kc in range(KC):
        tmp = tb.tmp.next()
        kb.op("dve", lambda e, tmp=tmp, kc=kc: e.tensor_tensor(out=tmp[:, :T], in0=tb.xt[:, kc, :T],
                                                                in1=tb.rstd[:, :T], op=ALU.mult),
              R=[tb.x[kc], tb.rstd], W=[tmp])
        kb.op("act", lambda e, tmp=tmp, kc=kc: e.activation(
            out=tb.ht[:, kc, :T], in_=tmp[:, :T], func=AF.Identity,
            scale=A[:, kc, r:r + 1], bias=modb[:, base + kc, r:r + 1]),
            R=[tmp, A, modb], W=[tb.h[kc]])


def emit_proj(kb, tb, T, w_d, J, epi, group=2):
    slabs = [(j0, min(group, J - j0)) for j0 in range(0, J, group)]
    loaded = {}

    def issue(i):
        j0, g = slabs[i]
        wb = tb.w8.next()
        stage_cast(kb, wb[:, 0:g, :, :], w_d[j0:j0 + g].rearrange("j p k n -> p j k n"), "p (j k n) -> p j k n", [wb], j=g, k=KC)
        loaded[i] = wb
    for i in range(min(2, len(slabs))):
        issue(i)
    for i, (j0, g) in enumerate(slabs):
        if i + 2 < len(slabs):
            issue(i + 2)
        wb = loaded.pop(i)
        pss = []
        for gi in range(g):
            ps = kb.psum()

            def mm(e, ps=ps, gi=gi, wb=wb):
                for kc in range(KC):
                    i_ = e.matmul(ps[:, :T], wb[:, gi, kc, :], tb.ht[:, kc, :T], start=(kc == 0), stop=(kc == KC - 1))
                return i_
            kb.op("pe", mm, R=[wb] + tb.h, W=[ps])
            pss.append(ps)
        epi(j0, pss)


def emit_ffn(kb, cst, tb, T, tabs, s, r, win_d, wout_d):
    emit_norm_mod(kb, cst, tb, T, tabs["A"][s], tabs["Bv"][s], r)

    def epi(j0, pss):
        j = j0 // 2
        sg = tb.tmp.next()
        kb.op("act", lambda e: e.activation(out=sg[:, :T], in_=pss[0][:, :T], func=AF.Silu), R=[pss[0]], W=[sg])
        kb.op("dve", lambda e: e.tensor_tensor(out=tb.at[:, j, :T], in0=pss[1][:, :T], in1=sg[:, :T], op=ALU.mult),
              R=[pss[1], sg], W=[tb.a[j]])
    emit_proj(kb, tb, T, win_d, 2 * FC, epi, group=2)
    G = tabs["G"][s]
    loaded = {}

    def issue(m):
        wb = tb.w22.next()
        stage_cast(kb, wb[:, :, :], wout_d[m], "p (k n) -> p k n", [wb], k=FC)
        loaded[m] = wb
    issue(0)
    issue(1)
    for m in range(KC):
        if m + 2 < KC:
            issue(m + 2)
        wb = loaded.pop(m)
        ps = kb.psum()

        def mm(e, ps=ps, wb=wb):
            for k in range(FC):
                i = e.matmul(ps[:, :T], wb[:, k, :], tb.at[:, k, :T], start=(k == 0), stop=(k == FC - 1))
            return i
        kb.op("pe", mm, R=[wb] + tb.a, W=[ps])
        kb.op("dve", lambda e, ps=ps, m=m: e.scalar_tensor_tensor(
            out=tb.xt[:, m, :T], in0=ps[:, :T], scalar=G[:, m, r:r + 1], in1=tb.xt[:, m, :T],
            op0=ALU.mult, op1=ALU.add), R=[ps, G, tb.x[m]], W=[tb.x[m]])


def emit_load_h(kb, tb, src_d, t0, T):
    v = src_d.rearrange("(kc p) t -> p kc t", p=128)
    for k0 in range(0, KC, 4):
        stage_cast(kb, tb.ht[:, k0:k0 + 4, :T], v[:, k0:k0 + 4, t0:t0 + T], "p (k t) -> p k t", tb.h[k0:k0 + 4], k=4)


def emit_post_proj(kb, tb, T, w_d, G, r, GB=None):
    if GB is not None:
        kb.op("dve", lambda e: e.tensor_tensor(
            out=tb.xt[:, :, :T], in0=tb.xt[:, :, :T], in1=GB[:, :, r:r + 1].broadcast_to([128, KC, T]), op=ALU.add),
            R=tb.x + [GB], W=tb.x)

    def epi(j0, pss):
        for gi, ps in enumerate(pss):
            m = j0 + gi
            kb.op("dve", lambda e, ps=ps, m=m: e.scalar_tensor_tensor(
                out=tb.xt[:, m, :T], in0=ps[:, :T], scalar=G[:, m, r:r + 1], in1=tb.xt[:, m, :T],
                op0=ALU.mult, op1=ALU.add), R=[ps, G, tb.x[m]], W=[tb.x[m]])
    emit_proj(kb, tb, T, w_d, KC, epi, group=2)


def emit_gb(kb, G, bias, name):
    GB = kb.sb([128, KC, 2], F32, name)
    kb.op("dve", lambda e: e.tensor_tensor(out=GB[:, :, :], in0=G[:, :, :],
                                           in1=bias[:, :, None].broadcast_to([128, KC, 2]), op=ALU.mult),
          R=[G, bias], W=[GB])
    return GB


class QkvBufs:
    def __init__(self, kb, wv_d):
        self.wv = kb.sb([128, KC, D], BF16, "wv")
        for k0 in range(0, KC, 2):
            stage_cast(kb, self.wv[:, k0:k0 + 2, :], wv_d[:, k0:k0 + 2, :], "p (k n) -> p k n", [self.wv], k=2)
        self.ct = kb.ring(2, [128, 512], F32, "ropec")
        self.st = kb.ring(2, [128, 512], F32, "ropes")
        self.qo = kb.ring(3, [128, 512], BF16, "qo")
        self.vo = kb.ring(3, [128, 512], BF16, "vo")


def emit_qkv(kb, tb, qb, T, t0, wqk_d, ctab_d, stab_d, QT_d, KT_d, V_d):
    ct = qb.ct.next()
    st = qb.st.next()
    kb.dma("sp", ct[:, :T], ctab_d[:, t0:t0 + T], W=[ct])
    kb.dma("sp", st[:, :T], stab_d[:, t0:t0 + T], W=[st])

    def epi(j0, pss):
        hd = j0 // 4
        which = (j0 % 4) // 2
        dst = QT_d if which == 0 else KT_d
        pa, pb = pss[0], pss[1]
        t1 = tb.tmp.next()
        kb.op("dve", lambda e: e.tensor_tensor(out=t1[:, :T], in0=pa[:, :T], in1=ct[:, :T], op=ALU.mult),
              R=[pa, ct], W=[t1])
        t2 = tb.tmp.next()
        kb.op("dve", lambda e: e.tensor_tensor(out=t2[:, :T], in0=pb[:, :T], in1=st[:, :T], op=ALU.mult),
              R=[pb, st], W=[t2])
        qo = qb.qo.next()
        kb.op("dve", lambda e: e.tensor_tensor(out=qo[:, :T], in0=t1[:, :T], in1=t2[:, :T], op=ALU.add),
              R=[t1, t2], W=[qo])
        kb.dma("sp", dst[hd, :, t0:t0 + T], qo[:, :T], R=[qo], is_output=True)
    import os as _os
    _m = _os.environ.get("QKV_MODE", "all")
    if _m in ("all", "qk"):
        emit_proj(kb, tb, T, wqk_d, 32, epi, group=2)
    if _m not in ("all", "v"):
        return
    nblk = (T + 127) // 128
    for tblk in range(nblk):
        M = min(128, T - 128 * tblk)
        for half in range(2):
            ps = kb.psum()

            def mm(e, ps=ps, tblk=tblk, M=M, half=half):
                for kc in range(KC):
                    i = e.matmul(ps[:M, :], tb.ht[:, kc, tblk * 128:tblk * 128 + M],
                                 qb.wv[:, kc, half * 512:(half + 1) * 512], start=(kc == 0), stop=(kc == KC - 1))
                return i
            kb.op("pe", mm, R=[qb.wv] + tb.h, W=[ps])
            vo = qb.vo.next()
            kb.op("act", lambda e, ps=ps, vo=vo, M=M: e.activation(out=vo[:M, :], in_=ps[:M, :], func=AF.Copy),
                  R=[ps], W=[vo])
            kb.dma("sp", V_d[t0 + tblk * 128:t0 + tblk * 128 + M, half * 512:(half + 1) * 512], vo[:M, :],
                   R=[vo], is_output=True)


def emit_final_norm(kb, cst, tb, T, fg, out_v, t0):
    ps = kb.psum()
    for kc in range(KC):
        sq = tb.sq.next()
        kb.op("act", lambda e, sq=sq, kc=kc: e.activation(out=sq[:, :T], in_=tb.xt[:, kc, :T], func=AF.Square),
              R=[tb.x[kc]], W=[sq])
        kb.op("pe", lambda e, sq=sq, kc=kc: e.matmul(ps[:, :T], cst.ones16[:, :], sq[:, :T],
                                                      start=(kc == 0), stop=(kc == KC - 1)),
              R=[sq, cst.ones16], W=[ps])
    kb.op("act", lambda e: e.activation(out=tb.rstd[:, :T], in_=ps[:, :T], func=AF.Sqrt, scale=1.0 / D, bias=cst.eps[:, 0:1]),
          R=[ps, cst.eps], W=[tb.rstd])
    kb.op("dve", lambda e: e.reciprocal(out=tb.rstd[:, :T], in_=tb.rstd[:, :T]), R=[tb.rstd], W=[tb.rstd])
    for kc in range(KC):
        kb.op("dve", lambda e, kc=kc: e.scalar_tensor_tensor(
            out=tb.xt[:, kc, :T], in0=tb.xt[:, kc, :T], scalar=fg[:, kc:kc + 1], in1=tb.rstd[:, :T],
            op0=ALU.mult, op1=ALU.mult), R=[tb.x[kc], tb.rstd, fg], W=[tb.x[kc]])
    kb.dma("sp", out_v[:, :, t0:t0 + T], tb.xt[:, :, :T], R=tb.x, is_output=True)


def build_attn(S, lam_init, ctx_out):
    kb = KB()
    NK = S + CTX
    NKB = NK // 128
    NQ = S + (CTX if ctx_out else 0)
    QT = kb.dram("QT", [2, 128, S + CTX], BF16)
    KT = kb.dram("KT", [2, 128, NK], BF16)
    V = kb.dram("V", [2, NK, 128], BF16)
    lamv = kb.dram("lamv", [4, 64], F32)
    subg = kb.dram("subg", [128, 1], F32)
    OT = kb.dram("OT", [2, 128, NQ], F32, kind="ExternalOutput")
    cst = emit_consts(kb)
    eps = kb.sb([128, 1], F32, "eps_a")
    kb.op("dve", lambda e: e.memset(eps[:, :], EPS), W=[eps])
    lv = kb.sb([128, 4, 64], F32, "lv")
    kb.dma("sp", lv[:, :, :], bass.AP(lamv.tensor, 0, [[0, 128], [64, 4], [1, 64]]), W=[lv])
    pr = kb.sb([128, 2, 64], F32, "lpr")
    kb.op("dve", lambda e: e.tensor_tensor(out=pr[:, 0, :], in0=lv[:, 0, :], in1=lv[:, 1, :], op=ALU.mult), R=[lv], W=[pr])
    kb.op("dve", lambda e: e.tensor_tensor(out=pr[:, 1, :], in0=lv[:, 2, :], in1=lv[:, 3, :], op=ALU.mult), R=[lv, pr], W=[pr])
    ss = kb.sb([128, 2], F32, "lss")
    kb.op("dve", lambda e: e.reduce_sum(out=ss[:, :], in_=pr[:, :, :], axis=mybir.AxisListType.X), R=[pr], W=[ss])
    ex = kb.sb([128, 2], F32, "lex")
    kb.op("act", lambda e: e.activation(out=ex[:, :], in_=ss[:, :], func=AF.Exp), R=[ss], W=[ex])
    nlam = kb.sb([128, 1], F32, "nlam")
    kb.op("dve", lambda e: e.tensor_tensor(out=nlam[:, :], in0=ex[:, 1:2], in1=ex[:, 0:1], op=ALU.subtract), R=[ex], W=[nlam])
    kb.op("dve", lambda e: e.tensor_scalar(out=nlam[:, :], in0=nlam[:, :], scalar1=-float(lam_init), scalar2=None, op0=ALU.add),
          R=[nlam], W=[nlam])
    sg = kb.sb([128, 1], F32, "subg_sb")
    kb.dma("sp", sg[:, :], subg, W=[sg])
    gsc = kb.sb([128, 1], F32, "gsc")
    kb.op("dve", lambda e: e.tensor_scalar(out=gsc[:, :], in0=sg[:, :], scalar1=float(1.0 - lam_init), scalar2=None, op0=ALU.mult),
          R=[sg], W=[gsc])

    kt = kb.sb([128, NK], BF16, "kt")
    vt = kb.sb([128, NKB, 128], BF16, "vt")
    qring = kb.ring(2, [128, 512], BF16, "qt")
    pring = kb.ring(6, [128, 512], BF16, "pt")
    fr = kb.ring(8, [128, 512], F32, "ef")
    banks = kb.psum_banks
    O = [banks[0], banks[1]]
    R_ = [banks[2], banks[3]]
    for b in range(2):
        kb.dma("sp", kt[:, :], KT[b], W=[kt])
        kb.dma("sp", vt[:, :, :], V[b].rearrange("(n p) e -> p n e", p=128), W=[vt])
        qtiles = [(q0, 512, 0, NKB) for q0 in range(0, S, 512)]
        if ctx_out:
            qtiles.append((S, CTX, S // 128, NKB))
        for (q0, TQ, kb0, kb1) in qtiles:
            qt = qring.next()
            kb.dma("sp", qt[:, :TQ], QT[b][:, q0:q0 + TQ], W=[qt])
            nkb = kb1 - kb0

            def emit_S(i):
                for m in range(2):
                    Sb = banks[4 + (2 * i + m) % 4]
                    blk = kb0 + i
                    kb.op("pe", lambda e, Sb=Sb, m=m, blk=blk: e.matmul(
                        Sb[:, :TQ], kt[64 * m:64 * m + 64, blk * 128:(blk + 1) * 128], qt[64 * m:64 * m + 64, :TQ],
                        start=True, stop=True), R=[kt, qt], W=[Sb])
            emit_S(0)
            for i in range(nkb):
                if i + 1 < nkb:
                    emit_S(i + 1)
                for m in range(2):
                    Sb = banks[4 + (2 * i + m) % 4]
                    P = pring.next()
                    kb.op("act", lambda e, Sb=Sb, P=P: e.activation(out=P[:, :TQ], in_=Sb[:, :TQ], func=AF.Exp, scale=0.125),
                          R=[Sb], W=[P])

                    if m == 1:
                        def pv(e, m=m, P=P, i=i):
                            e.matmul(O[m][:, :TQ], vt[:, kb0 + i, :], P[:, :TQ], start=(i == 0), stop=(i == nkb - 1))
                            return e.matmul(R_[m][:, :TQ], cst.ones16[:, :], P[:, :TQ], start=(i == 0), stop=(i == nkb - 1))
                        kb.op("pe", pv, R=[P, vt, cst.ones16], W=[O[m], R_[m]])
                    else:
                        kb.op("pe", lambda e, m=m, P=P, i=i: e.matmul(
                            O[m][:, :TQ], vt[:, kb0 + i, :], P[:, :TQ], start=(i == 0), stop=(i == nkb - 1)),
                            R=[P, vt], W=[O[m]])
                        if i == 0:
                            kb.op("dve", lambda e, P=P: e.tensor_copy(out=R_[0][:, :TQ], in_=P[:, :TQ]), R=[P], W=[R_[0]])
                        else:
                            kb.op("dve", lambda e, P=P: e.tensor_tensor(out=R_[0][:, :TQ], in0=R_[0][:, :TQ], in1=P[:, :TQ], op=ALU.add),
                                  R=[P, R_[0]], W=[R_[0]])
            accs = fr.next()
            kb.op("act", lambda e: e.activation(out=accs[:, :TQ], in_=R_[0][:, :TQ], func=AF.Copy), R=[R_[0]], W=[accs])
            R0b = banks[5]
            kb.op("pe", lambda e: e.matmul(R0b[:, :TQ], cst.ones32[:, :], accs[:, :TQ], start=True, stop=True),
                  R=[accs, cst.ones32], W=[R0b])
            r0 = fr.next()
            kb.op("dve", lambda e: e.reciprocal(out=r0[:, :TQ], in_=R0b[:, :TQ]), R=[R0b], W=[r0])
            r1 = fr.next()
            kb.op("dve", lambda e: e.reciprocal(out=r1[:, :TQ], in_=R_[1][:, :TQ]), R=[R_[1]], W=[r1])
            kb.op("dve", lambda e: e.tensor_scalar(out=r1[:, :TQ], in0=r1[:, :TQ], scalar1=nlam[:, 0:1], scalar2=None, op0=ALU.mult),
                  R=[r1, nlam], W=[r1])
            o0 = fr.next()
            kb.op("dve", lambda e: e.tensor_tensor(out=o0[:, :TQ], in0=O[0][:, :TQ], in1=r0[:, :TQ], op=ALU.mult),
                  R=[O[0], r0], W=[o0])
            o1 = fr.next()
            kb.op("dve", lambda e: e.tensor_tensor(out=o1[:, :TQ], in0=O[1][:, :TQ], in1=r1[:, :TQ], op=ALU.mult),
                  R=[O[1], r1], W=[o1])
            kb.op("dve", lambda e: e.tensor_tensor(out=o0[:, :TQ], in0=o0[:, :TQ], in1=o1[:, :TQ], op=ALU.add),
                  R=[o0, o1], W=[o0])
            sq = fr.next()
            kb.op("act", lambda e: e.activation(out=sq[:, :TQ], in_=o0[:, :TQ], func=AF.Square), R=[o0], W=[sq])
            Sb = banks[4]
            kb.op("pe", lambda e: e.matmul(Sb[:, :TQ], cst.ones32[:, :], sq[:, :TQ], start=True, stop=True),
                  R=[sq, cst.ones32], W=[Sb])
            rs = fr.next()
            kb.op("act", lambda e: e.activation(out=rs[:, :TQ], in_=Sb[:, :TQ], func=AF.Sqrt, scale=1.0 / 128, bias=eps[:, 0:1]),
                  R=[Sb, eps], W=[rs])
            kb.op("dve", lambda e: e.reciprocal(out=rs[:, :TQ], in_=rs[:, :TQ]), R=[rs], W=[rs])
            ob = fr.next()
            kb.op("dve", lambda e: e.scalar_tensor_tensor(out=ob[:, :TQ], in0=o0[:, :TQ], scalar=gsc[:, 0:1], in1=rs[:, :TQ],
                                                          op0=ALU.mult, op1=ALU.mult), R=[o0, gsc, rs], W=[ob])
            kb.dma("sp", OT[b][:, q0:q0 + TQ], ob[:, :TQ], R=[ob], is_output=True)
    return kb.finish()


def _decl_layer(kb, L):
    return dict(
        wmod=kb.dram(f"wmod{L}", [72, 128, KC, 128], F32),
        bmod=kb.dram(f"bmod{L}", [128, 72], F32),
        normg=kb.dram(f"normg{L}", [128, 3, KC], F32),
    )


def _decl_ffn(kb, L, k):
    return (kb.dram(f"win{L}_{k}", [2 * FC, 128, KC, 128], F32), kb.dram(f"wout{L}_{k}", [KC, 128, FC, 128], F32))


def build_tok(kind, S):
    TL = S // 4
    NT = TL + 64
    kb = KB()
    xT = kb.dram("xT", [D, NT], F32)
    cT = kb.dram("cT", [128, KC, 2], F32)
    xv = xT.rearrange("(kc p) t -> p kc t", p=128)
    layers = {"A": [0], "C": [0, 1], "E1": [1, 2], "E2": [2, 3], "F": [3]}[kind]
    ld = {L: _decl_layer(kb, L) for L in layers}
    cst = emit_consts(kb)
    tabs = {L: emit_mod(kb, cst, cT, ld[L]["wmod"], ld[L]["bmod"], ld[L]["normg"], f"m{L}") for L in layers}
    tb = TokBufs(kb)
    tiles = [(t0, 512, 0) for t0 in range(0, TL, 512)] + [(TL, 64, 1)]
    import os as _os
    if _os.environ.get("TILES_MODE") == "lat":
        tiles = tiles[:-1]
    if _os.environ.get("TILES_MODE") == "ctx":
        tiles = tiles[-1:]
    if kind != "F":
        xo = kb.dram("xo", [D, NT], F32, kind="ExternalOutput")
        xov = xo.rearrange("(kc p) t -> p kc t", p=128)

    def load_x(t0, T):
        kb.dma("sp", tb.xt[:, :, :T], xv[:, :, t0:t0 + T], W=tb.x)

    def store_x(t0, T):
        kb.dma("sp", xov[:, :, t0:t0 + T], tb.xt[:, :, :T], R=tb.x, is_output=True)

    def small(name, shape):
        d = kb.dram(name, shape, F32)
        b = kb.sb(shape, F32, name + "_sb")
        kb.dma("sp", b.t[tuple(slice(None) for _ in shape)], d, W=[b])
        return b

    if kind in ("A", "E2"):
        La = 0 if kind == "A" else 3
        wqk = kb.dram("wqk", [32, 128, KC, 128], F32)
        wv = kb.dram("wv", [128, KC, D], F32)
        ctab = kb.dram("ctab", [128, NT], F32)
        stab = kb.dram("stab", [128, NT], F32)
        QT = kb.dram("QT", [8, 128, NT], BF16, kind="ExternalOutput")
        KT = kb.dram("KT", [8, 128, NT], BF16, kind="ExternalOutput")
        Vd = kb.dram("V", [NT, D], BF16, kind="ExternalOutput")
        qb = QkvBufs(kb, wv)

    if kind == "A":
        f0 = _decl_ffn(kb, 0, 0)
        for (t0, T, r) in tiles:
            load_x(t0, T)
            emit_ffn(kb, cst, tb, T, tabs[0], 0, r, *f0)
            store_x(t0, T)
            emit_norm_mod(kb, cst, tb, T, tabs[0]["A"][1], tabs[0]["Bv"][1], r)
            emit_qkv(kb, tb, qb, T, t0, wqk, ctab, stab, QT, KT, Vd)

    elif kind == "C":
        oT = kb.dram("oT", [D, NT], F32)
        wo = kb.dram("wo", [KC, 128, KC, 128], F32)
        f02 = _decl_ffn(kb, 0, 1)
        f10 = _decl_ffn(kb, 1, 0)
        whin = kb.dram("whin", [24, 128, KC, 128], F32)
        bhin = small("bhin", [128, 24])
        uT = kb.dram("uT", [3 * D, NT], F32, kind="ExternalOutput")
        uo = kb.ring(3, [128, 512], F32, "uo")
        for (t0, T, r) in tiles:
            load_x(t0, T)
            emit_load_h(kb, tb, oT, t0, T)
            emit_post_proj(kb, tb, T, wo, tabs[0]["G"][1], r)
            emit_ffn(kb, cst, tb, T, tabs[0], 2, r, *f02)
            emit_ffn(kb, cst, tb, T, tabs[1], 0, r, *f10)
            store_x(t0, T)
            emit_norm_mod(kb, cst, tb, T, tabs[1]["A"][1], tabs[1]["Bv"][1], r)

            def epi(j0, pss, t0=t0, T=T):
                for gi, ps in enumerate(pss):
                    j = j0 + gi
                    ub = uo.next()
                    kb.op("act", lambda e, ps=ps, ub=ub, j=j: e.activation(
                        out=ub[:, :T], in_=ps[:, :T], func=AF.Identity, bias=bhin[:, j:j + 1]), R=[ps, bhin], W=[ub])
                    kb.dma("sp", uT[j * 128:(j + 1) * 128, t0:t0 + T], ub[:, :T], R=[ub], is_output=True)
            emit_proj(kb, tb, T, whin, 24, epi, group=2)

    elif kind == "E1":
        zT = kb.dram("zT", [D, NT], F32)
        who = kb.dram("who", [KC, 128, KC, 128], F32)
        bho = small("bho", [128, KC])
        f12 = _decl_ffn(kb, 1, 1)
        f20 = _decl_ffn(kb, 2, 0)
        wpw1 = kb.dram("wpw1", [16, 128, KC, 128], F32)
        bpw1 = small("bpw1", [128, 16])
        gT = kb.dram("gT", [D, NT], F32, kind="ExternalOutput")
        uo = kb.ring(3, [128, 512], F32, "uo")
        GB = emit_gb(kb, tabs[1]["G"][1], bho, "GBho")
        for (t0, T, r) in tiles:
            load_x(t0, T)
            emit_load_h(kb, tb, zT, t0, T)
            emit_post_proj(kb, tb, T, who, tabs[1]["G"][1], r, GB)
            emit_ffn(kb, cst, tb, T, tabs[1], 2, r, *f12)
            emit_ffn(kb, cst, tb, T, tabs[2], 0, r, *f20)
            store_x(t0, T)
            emit_norm_mod(kb, cst, tb, T, tabs[2]["A"][1], tabs[2]["Bv"][1], r)

            def epi(j0, pss, t0=t0, T=T):
                for gi in range(0, len(pss), 2):
                    j = (j0 + gi) // 2
                    pa, pg = pss[gi], pss[gi + 1]
                    sgm = tb.tmp.next()
                    kb.op("act", lambda e, pg=pg, sgm=sgm, j=j: e.activation(
                        out=sgm[:, :T], in_=pg[:, :T], func=AF.Sigmoid, bias=bpw1[:, 2 * j + 1:2 * j + 2]),
                        R=[pg, bpw1], W=[sgm])
                    ub = uo.next()
                    kb.op("dve", lambda e, pa=pa, sgm=sgm, ub=ub, j=j: e.scalar_tensor_tensor(
                        out=ub[:, :T], in0=pa[:, :T], scalar=bpw1[:, 2 * j:2 * j + 1], in1=sgm[:, :T],
                        op0=ALU.add, op1=ALU.mult), R=[pa, sgm, bpw1], W=[ub])
                    kb.dma("sp", gT[j * 128:(j + 1) * 128, t0:t0 + T], ub[:, :T], R=[ub], is_output=True)
            emit_proj(kb, tb, T, wpw1, 16, epi, group=2)

    elif kind == "E2":
        NH = TL + 30 + 64 + 30
        gH = kb.dram("gH", [D, NH], F32)
        ghv = gH.rearrange("(kc p) t -> p kc t", p=128)
        wdw = small("wdw", [128, KC, 31])
        bdw = small("bdw", [128, KC])
        lng = small("lng", [128, KC])
        lnb = small("lnb", [128, KC])
        wpw2 = kb.dram("wpw2", [KC, 128, KC, 128], F32)
        bpw2 = small("bpw2", [128, KC])
        f22 = _decl_ffn(kb, 2, 1)
        f30 = _decl_ffn(kb, 3, 0)
        GB = emit_gb(kb, tabs[2]["G"][1], bpw2, "GBpw2")
        lneps = kb.sb([128, 1], F32, "lneps")
        kb.op("dve", lambda e: e.memset(lneps[:, :], 1e-5), W=[lneps])
        gt_t = kb.sbt([128, KC, 512 + 30], F32, "gt")
        gt = [Buf(gt_t, f"gt{k}") for k in range(KC)]
        cu_t = kb.sbt([128, KC, 512], F32, "cu")
        cu = [Buf(cu_t, f"cu{k}") for k in range(KC)]
        mt = kb.sb([128, 512], F32, "lnmean")
        msq = kb.sb([128, 512], F32, "lnmsq")
        for (t0, T, r) in tiles:
            load_x(t0, T)
            c0 = t0 if r == 0 else TL + 30
            kb.dma("sp", gt_t[:, :, :T + 30], ghv[:, :, c0:c0 + T + 30], W=gt)
            ps1 = kb.psum()
            ps2 = kb.psum()
            for kc in range(KC):
                kb.op("dve", lambda e, kc=kc: e.tensor_scalar(
                    out=cu_t[:, kc, :T], in0=gt_t[:, kc, 0:T], scalar1=wdw[:, kc, 0:1], scalar2=bdw[:, kc:kc + 1],
                    op0=ALU.mult, op1=ALU.add), R=[gt[kc], wdw, bdw], W=[cu[kc]])
                for k in range(1, 31):
                    kb.op("dve", lambda e, kc=kc, k=k: e.scalar_tensor_tensor(
                        out=cu_t[:, kc, :T], in0=gt_t[:, kc, k:k + T], scalar=wdw[:, kc, k:k + 1], in1=cu_t[:, kc, :T],
                        op0=ALU.mult, op1=ALU.add), R=[gt[kc], wdw, cu[kc]], W=[cu[kc]])
                sq = tb.sq.next()
                kb.op("act", lambda e, sq=sq, kc=kc: e.activation(out=sq[:, :T], in_=cu_t[:, kc, :T], func=AF.Square),
                      R=[cu[kc]], W=[sq])
                kb.op("pe", lambda e, kc=kc: e.matmul(ps1[:, :T], cst.ones32[:, :], cu_t[:, kc, :T],
                                                      start=(kc == 0), stop=(kc == KC - 1)), R=[cu[kc], cst.ones32], W=[ps1])
                kb.op("pe", lambda e, sq=sq, kc=kc: e.matmul(ps2[:, :T], cst.ones16[:, :], sq[:, :T],
                                                             start=(kc == 0), stop=(kc == KC - 1)), R=[sq, cst.ones16], W=[ps2])
            kb.op("act", lambda e: e.activation(out=mt[:, :T], in_=ps1[:, :T], func=AF.Copy, scale=1.0 / D), R=[ps1], W=[mt])
            kb.op("dve", lambda e: e.tensor_tensor(out=msq[:, :T], in0=mt[:, :T], in1=mt[:, :T], op=ALU.mult), R=[mt], W=[msq])
            kb.op("dve", lambda e: e.scalar_tensor_tensor(out=msq[:, :T], in0=ps2[:, :T], scalar=1.0 / D, in1=msq[:, :T],
                                                          op0=ALU.mult, op1=ALU.subtract), R=[ps2, msq], W=[msq])
            kb.op("act", lambda e: e.activation(out=tb.rstd[:, :T], in_=msq[:, :T], func=AF.Sqrt, bias=lneps[:, 0:1]),
                  R=[msq, lneps], W=[tb.rstd])
            kb.op("dve", lambda e: e.reciprocal(out=tb.rstd[:, :T], in_=tb.rstd[:, :T]), R=[tb.rstd], W=[tb.rstd])
            for kc in range(KC):
                d1 = tb.tmp.next()
                kb.op("dve", lambda e, kc=kc, d1=d1: e.tensor_tensor(out=d1[:, :T], in0=cu_t[:, kc, :T], in1=mt[:, :T], op=ALU.subtract),
                      R=[cu[kc], mt], W=[d1])
                kb.op("dve", lambda e, d1=d1: e.tensor_tensor(out=d1[:, :T], in0=d1[:, :T], in1=tb.rstd[:, :T], op=ALU.mult),
                      R=[d1, tb.rstd], W=[d1])
                kb.op("act", lambda e, kc=kc, d1=d1: e.activation(out=tb.ht[:, kc, :T], in_=d1[:, :T], func=AF.Silu,
                                                                  scale=lng[:, kc:kc + 1], bias=lnb[:, kc:kc + 1]),
                      R=[d1, lng, lnb], W=[tb.h[kc]])
            emit_post_proj(kb, tb, T, wpw2, tabs[2]["G"][1], r, GB)
            emit_ffn(kb, cst, tb, T, tabs[2], 2, r, *f22)
            emit_ffn(kb, cst, tb, T, tabs[3], 0, r, *f30)
            store_x(t0, T)
            emit_norm_mod(kb, cst, tb, T, tabs[3]["A"][1], tabs[3]["Bv"][1], r)
            emit_qkv(kb, tb, qb, T, t0, wqk, ctab, stab, QT, KT, Vd)

    elif kind == "F":
        oT = kb.dram("oT", [D, NT], F32)
        wo = kb.dram("wo", [KC, 128, KC, 128], F32)
        f32_ = _decl_ffn(kb, 3, 1)
        fg = small("fg", [128, KC])
        yT = kb.dram("yT", [D, TL], F32, kind="ExternalOutput")
        yv = yT.rearrange("(kc p) t -> p kc t", p=128)
        for (t0, T, r) in tiles:
            if r == 1:
                continue
            load_x(t0, T)
            emit_load_h(kb, tb, oT, t0, T)
            emit_post_proj(kb, tb, T, wo, tabs[3]["G"][1], r)
            emit_ffn(kb, cst, tb, T, tabs[3], 2, r, *f32_)
            emit_final_norm(kb, cst, tb, T, fg, yv, t0)
    return kb.finish()


TWO_PI = 2.0 * math.pi


def build_hyena(S):
    kb = KB()
    nc = kb.nc
    cst = emit_consts(kb)
    up_l = kb.dram("up_l", [2, 3, 128, S], F32)
    up_c = kb.dram("up_c", [2, 3, 128, CTX], F32)
    fw1 = kb.dram("fw1", [33, 64], F32)
    fw2 = kb.dram("fw2", [64, 64], F32)
    fw3 = kb.dram("fw3", [64, 64], F32)
    fw4 = kb.dram("fw4", [64, 2, 128], F32)
    fbf = kb.dram("fbf", [64, 4], F32)
    wsh = kb.dram("wsh", [128, 3, 3], F32)
    bsh = kb.dram("bsh", [128, 3], F32)
    skp = kb.dram("skp", [128, 1], F32)
    dlt = kb.dram("dlt", [128, 1], F32)
    ident = kb.dram("ident", [128, 128], F32)
    antiid = kb.dram("antiid", [128, 128], F32)
    emb = {S: kb.dram("emb_l", [33, 2 * S], F32), CTX: kb.dram("emb_c", [33, 2 * CTX], F32)}
    tex = {S: kb.dram("tex_l", [1, 2 * S], F32), CTX: kb.dram("tex_c", [1, 2 * CTX], F32)}
    KK = {S: kb.dram("KK_l", [128, 2 * S], BF16, kind="ExternalOutput"),
          CTX: kb.dram("KK_c", [128, 2 * CTX], BF16, kind="ExternalOutput")}
    zo = {S: kb.dram("z_l", [2, 128, S], F32, kind="ExternalOutput"),
          CTX: kb.dram("z_c", [2, 128, CTX], F32, kind="ExternalOutput")}
    up = {S: up_l, CTX: up_c}

    def small(d, shape, name):
        b = kb.sb(shape, F32, name)
        kb.dma("sp", b.t[tuple(slice(None) for _ in shape)], d, W=[b])
        return b
    w1 = small(fw1, [33, 64], "w1s")
    w2 = small(fw2, [64, 64], "w2s")
    w3 = small(fw3, [64, 64], "w3s")
    w4 = small(fw4, [64, 2, 128], "w4s")
    bf_ = small(fbf, [64, 4], "bfs")
    ws = small(wsh, [128, 3, 3], "wss")
    bs = small(bsh, [128, 3], "bss")
    sk = small(skp, [128, 1], "sks")
    dl = small(dlt, [128, 1], "dls")
    idf = small(ident, [128, 128], "idf")
    idb = kb.sb([128, 128], BF16, "idb")
    kb.op("dve", lambda e: e.tensor_copy(out=idb[:, :], in_=idf[:, :]), R=[idf], W=[idb])
    jf = small(antiid, [128, 128], "jf")
    jb = kb.sb([128, 128], BF16, "jb")
    kb.op("dve", lambda e: e.tensor_copy(out=jb[:, :], in_=jf[:, :]), R=[jf], W=[jb])
    xsr = kb.ring(2, [128, 128], BF16, "xsr")
    ndl = kb.sb([128, 1], F32, "ndl")
    kb.op("dve", lambda e: e.tensor_scalar(out=ndl[:, :], in0=dl[:, :], scalar1=-1.0, scalar2=None, op0=ALU.mult), R=[dl], W=[ndl])
    scl = kb.sb([64, 1], F32, "scl")
    kb.op("dve", lambda e: e.tensor_scalar(out=scl[:, :], in0=bf_[:, 3:4], scalar1=1.0 / TWO_PI, scalar2=None, op0=ALU.mult),
          R=[bf_], W=[scl])
    bia = kb.sb([64, 3], F32, "bia")
    kb.op("dve", lambda e: e.tensor_scalar(out=bia[:, :], in0=bf_[:, 0:3], scalar1=scl[:, 0:1], scalar2=8.0,
                                           op0=ALU.mult, op1=ALU.add), R=[bf_, scl], W=[bia])

    embt = kb.ring(2, [33, 512], F32, "embt")
    text = kb.ring(2, [128, 512], F32, "text")
    hd = kb.ring(4, [64, 512], F32, "hd")
    hdi = kb.ring(2, [64, 512], mybir.dt.int32, "hdi")
    hdf = kb.ring(2, [64, 512], F32, "hdf")
    dec = kb.ring(2, [128, 512], F32, "dec")
    kko = kb.ring(3, [128, 512], BF16, "kko")
    kkbuf = Buf(None, "kkdram")
    def gen_filter(L):
        n2 = 2 * L
        for m0 in range(0, n2, 512):
            W_ = min(512, n2 - m0)
            et = embt.next()
            kb.dma("sp", et[:, :W_], emb[L][:, m0:m0 + W_], W=[et])
            tt = text.next()
            kb.dma("sp", tt[:, :W_], bass.AP(tex[L].tensor, m0, [[0, 128], [1, W_]]), W=[tt])
            cur, curK = et, 33
            for li, wl in enumerate((w1, w2, w3)):
                ps = kb.psum()
                kb.op("pe", lambda e, ps=ps, wl=wl, cur=cur, curK=curK: e.matmul(
                    ps[:64, :W_], wl[:curK, :], cur[:curK, :W_], start=True, stop=True), R=[wl, cur], W=[ps])
                u = hd.next()
                kb.op("act", lambda e, ps=ps, u=u, li=li: e.activation(
                    out=u[:, :W_], in_=ps[:64, :W_], func=AF.Identity, scale=scl[:, 0:1], bias=bia[:, li:li + 1]),
                    R=[ps, scl, bia], W=[u])
                ki = hdi.next()
                kb.op("dve", lambda e, u=u, ki=ki: e.tensor_copy(out=ki[:, :W_], in_=u[:, :W_]), R=[u], W=[ki])
                kf = hdf.next()
                kb.op("dve", lambda e, kf=kf, ki=ki: e.tensor_copy(out=kf[:, :W_], in_=ki[:, :W_]), R=[ki], W=[kf])
                kb.op("dve", lambda e, u=u, kf=kf: e.tensor_tensor(out=u[:, :W_], in0=u[:, :W_], in1=kf[:, :W_], op=ALU.subtract),
                      R=[u, kf], W=[u])
                kb.op("dve", lambda e, u=u, kf=kf: e.tensor_scalar(out=kf[:, :W_], in0=u[:, :W_], scalar1=0.5, scalar2=None, op0=ALU.is_ge),
                      R=[u], W=[kf])
                kb.op("dve", lambda e, u=u, kf=kf: e.tensor_tensor(out=u[:, :W_], in0=u[:, :W_], in1=kf[:, :W_], op=ALU.subtract),
                      R=[u, kf], W=[u])
                kb.op("act", lambda e, u=u: e.activation(out=u[:, :W_], in_=u[:, :W_], func=AF.Sin, scale=TWO_PI), R=[u], W=[u])
                cur, curK = u, 64
            dc = dec.next()
            kb.op("act", lambda e, dc=dc, tt=tt: e.activation(out=dc[:, :W_], in_=tt[:, :W_], func=AF.Exp, scale=ndl[:, 0:1]),
                  R=[tt, ndl], W=[dc])
            ko = kko.next()
            segs = []
            if m0 < L:
                segs.append((0, min(W_, L - m0), 1))
            if m0 + W_ > L:
                s0 = max(0, L - m0)
                segs.append((s0, W_, 0))
            for (a0, a1, dirn) in segs:
                ps = kb.psum()
                kb.op("pe", lambda e, ps=ps, cur=cur, a0=a0, a1=a1, dirn=dirn: e.matmul(
                    ps[:, a0:a1], w4[:, dirn, :], cur[:, a0:a1], start=True, stop=True), R=[w4, cur], W=[ps])
                kb.op("dve", lambda e, ps=ps, a0=a0, a1=a1: e.tensor_tensor(
                    out=ko[:, a0:a1], in0=ps[:, a0:a1], in1=dc[:, a0:a1], op=ALU.mult), R=[ps, dc, ko], W=[ko])
            kb.dma("sp", KK[L][:, m0:m0 + W_], ko[:, :W_], R=[ko, kkbuf], is_output=True)
            yield
    def kk_wait():
        for tk in kkbuf.readers:
            kb._wait("sp", tk)
            kb._wait("act", tk)

    TC = 512
    TG = 16
    U = kb.ring(2, [128, 3, TC + 2], F32, "U")
    cvt = [kb.ring(2, [128, TC], F32, f"cv{p}") for p in range(3)]
    vvb = kb.ring(2, [128, TC], BF16, "vvb")
    zt = kb.ring(2, [128, TC], F32, "zt")

    def load_conv(L, b, c0, W_, parts):
        u = U.next()
        lo = max(c0 - 1, 0)
        hi = min(c0 + W_ + 1, L)
        if c0 == 0:
            kb.op("dve", lambda e: e.memset(u[:, :, 0:1], 0.0), W=[u])
        if c0 + W_ == L:
            kb.op("dve", lambda e: e.memset(u[:, :, W_ + 1:W_ + 2], 0.0), W=[u])
        kb.dma("sp", u[:, :, lo - (c0 - 1):hi - (c0 - 1)], up[L][b][:, :, lo:hi].rearrange("q c t -> c q t"), R=[u], W=[u])
        out = {}
        for p in parts:
            cv = cvt[p].next()
            kb.op("dve", lambda e, p=p, cv=cv: e.tensor_scalar(
                out=cv[:, :W_], in0=u[:, p, 0:W_], scalar1=ws[:, p, 0:1], scalar2=bs[:, p:p + 1],
                op0=ALU.mult, op1=ALU.add), R=[u, ws, bs], W=[cv])
            for k in (1, 2):
                kb.op("dve", lambda e, p=p, cv=cv, k=k: e.scalar_tensor_tensor(
                    out=cv[:, :W_], in0=u[:, p, k:k + W_], scalar=ws[:, p, k:k + 1], in1=cv[:, :W_],
                    op0=ALU.mult, op1=ALU.add), R=[u, ws, cv], W=[cv])
            out[p] = cv
        return out

    NBmax = S // 128
    vvT = kb.sb([128, 2 * NBmax, 128], BF16, "vvT")
    yT = kb.sb([128, 2 * NBmax, 128], BF16, "yT")
    tg = kb.ring(3, [128, TG * 128], BF16, "tg")

    def gen_p1(L):
        nb = L // 128
        tcw = min(TC, L)
        for b in range(2):
            for c0 in range(0, L, tcw):
                cv = load_conv(L, b, c0, tcw, (1, 2))
                vb = vvb.next()
                kb.op("dve", lambda e, cv=cv, vb=vb: e.tensor_tensor(out=vb[:, :tcw], in0=cv[1][:, :tcw], in1=cv[2][:, :tcw], op=ALU.mult),
                      R=[cv[1], cv[2]], W=[vb])
                for bi in range(tcw // 128):
                    blk = c0 // 128 + bi
                    ps = kb.psum()
                    pv = ps.t[:, 0:64].bitcast(BF16)
                    kb.op("pe", lambda e, pv=pv, vb=vb, bi=bi: e.transpose(pv, vb[:, bi * 128:(bi + 1) * 128], idb[:, :]),
                          R=[vb, idb], W=[ps])
                    xs = xsr.next()
                    kb.op("act", lambda e, pv=pv, xs=xs: e.activation(out=xs[:, :], in_=pv, func=AF.Copy), R=[ps], W=[xs])
                    ps2 = kb.psum()
                    kb.op("pe", lambda e, ps2=ps2, xs=xs: e.matmul(ps2[:, 0:128], jb[:, :], xs[:, :], start=True, stop=True),
                          R=[jb, xs], W=[ps2])
                    kb.op("act", lambda e, ps2=ps2, b=b, blk=blk: e.activation(out=vvT[:, 2 * blk + b, :], in_=ps2[:, 0:128], func=AF.Copy),
                          R=[ps2], W=[vvT])
                yield

    def phase23(L):
        nb = L // 128
        tcw = min(TC, L)
        ND = 2 * nb - 1
        order = [nb - 1] + [d for d in range(ND) if d != nb - 1]
        for c in range(128):
            ps = kb.psum()
            psv = ps.t[:, 0:2 * nb]
            g0s = list(range(0, ND, TG))
            g0s.sort(key=lambda g0: 0 if g0 <= nb - 1 < g0 + TG else 1)
            for gi_, g0 in enumerate(g0s):
                g = min(TG, ND - g0)
                t = tg.next()
                d0 = g0 - (nb - 1)
                src = bass.AP(KK[L].tensor, c * 2 * L + L + 128 * d0 - 127, [[1, 128], [1, g * 128]])
                kb.dma("sp" if gi_ % 2 == 0 else "act", t[:, :g * 128], src, W=[t])
                dis = [d for d in order if g0 <= d < g0 + g]
                lastg = (gi_ == len(g0s) - 1)

                def mm(e, t=t, g0=g0, dis=dis, lastg=lastg):
                    i = None
                    for n_, di in enumerate(dis):
                        dlt_ = di - (nb - 1)
                        b0, b1 = max(0, -dlt_), min(nb, nb - dlt_)
                        i = e.matmul(psv[:, 2 * (b0 + dlt_):2 * (b1 + dlt_)], t[:, (di - g0) * 128:(di - g0 + 1) * 128],
                                     vvT[:, 2 * b0:2 * b1, c], start=(dlt_ == 0), stop=(lastg and n_ == len(dis) - 1),
                                     skip_group_check=True)
                    return i
                kb.op("pe", mm, R=[t, vvT], W=[ps])
            kb.op("act", lambda e, psv=psv, c=c: e.activation(out=yT[:, 0:2 * nb, c], in_=psv, func=AF.Copy), R=[ps], W=[yT])
        for b in range(2):
            for c0 in range(0, L, tcw):
                cv = load_conv(L, b, c0, tcw, (0, 1, 2))
                vv = cv[1]
                kb.op("dve", lambda e, cv=cv: e.tensor_tensor(out=cv[1][:, :tcw], in0=cv[1][:, :tcw], in1=cv[2][:, :tcw], op=ALU.mult),
                      R=[cv[1], cv[2]], W=[cv[1]])
                ps = kb.psum()
                pv = ps.t[:, 0:512].bitcast(BF16)
                for bi in range(tcw // 128):
                    blk = c0 // 128 + bi
                    kb.op("pe", lambda e, pv=pv, bi=bi, b=b, blk=blk: e.transpose(
                        pv[:, bi * 128:(bi + 1) * 128], yT[:, 2 * blk + b, :], idb[:, :]), R=[yT, idb], W=[ps])
                z = zt.next()
                kb.op("dve", lambda e, pv=pv, vv=vv, z=z: e.scalar_tensor_tensor(
                    out=z[:, :tcw], in0=vv[:, :tcw], scalar=sk[:, 0:1], in1=pv[:, :tcw], op0=ALU.mult, op1=ALU.add),
                    R=[vv, sk, ps], W=[z])
                kb.op("dve", lambda e, z=z, cv=cv: e.tensor_tensor(out=z[:, :tcw], in0=z[:, :tcw], in1=cv[0][:, :tcw], op=ALU.mult),
                      R=[z, cv[0]], W=[z])
                kb.dma("sp", zo[L][b][:, c0:c0 + tcw], z[:, :tcw], R=[z], is_output=True)
    def drain(*gens):
        gens = [g for g in gens]
        while gens:
            for g in list(gens):
                try:
                    next(g)
                except StopIteration:
                    gens.remove(g)
    drain(gen_filter(S), gen_p1(S))
    drain(gen_filter(CTX))
    kk_wait()
    phase23(S)
    drain(gen_p1(CTX))
    phase23(CTX)
    return kb.finish()


def _rope_tables(S):
    n = np.arange(S)
    row = (n // GRID_W).astype(np.float32)
    col = (n % GRID_W).astype(np.float32)
    inv = (np.float32(10000.0) ** (-(2.0 * np.arange(16, dtype=np.float32)) / 32)).astype(np.float32)
    C = np.zeros((64, S), np.float32)
    Sg = np.zeros((64, S), np.float32)
    for d in range(64):
        ang = ((row if d < 32 else col) * inv[d % 16]).astype(np.float32)
        C[d] = np.cos(ang)
        Sg[d] = np.sin(ang) * (-1.0 if (d % 32) < 16 else 1.0)
    return np.concatenate([C, C], 0), np.concatenate([Sg, Sg], 0)


def _hy_tables(L):
    f32 = np.float32
    t = np.linspace(0.0, 1.0, L, dtype=f32)
    wpos = (f32(2.0 * math.pi / L) * np.arange(L, dtype=f32)).astype(f32)
    bands = np.linspace(1e-4, 15.0, 16, dtype=f32)
    fw = wpos[:, None] * bands[None, :]
    emb = np.concatenate([t[:, None], np.cos(fw), -np.sin(fw)], -1).astype(f32)
    m = np.arange(2 * L)
    idx = np.where(m >= L, m - L, L - m)
    idx[0] = 0
    return np.ascontiguousarray(emb[idx].T), np.ascontiguousarray(t[idx][None, :])


def _ffn_w(w_in, w_out):
    order = []
    for j in range(FC):
        order += [j, FC + j]
    return tile_w(w_in, order), tile_w(w_out)


_PROGS = {}


def _prog(key, fn):
    if key not in _PROGS:
        _PROGS[key] = fn()
    return _PROGS[key]


def _launch(nc, in_maps):
    res = run_bass_kernel_spmd(nc, in_maps, core_ids=list(range(NCORE)))
    return res.results


def _forward(S, x, c, ctx, c_ctx, w_mod, b_mod, norm_g, w_ffn_in, w_ffn_out,
             attn_w_qkv, attn_w_o, attn_lambda, attn_subln_g,
             hy_w_in, hy_b_in, hy_w_short, hy_b_short, hy_f_w1, hy_f_b1, hy_f_w2, hy_f_b2,
             hy_f_w3, hy_f_b3, hy_f_freq, hy_f_w4, hy_skip, hy_w_out, hy_b_out,
             cv_w_pw1, cv_b_pw1, cv_w_dw, cv_b_dw, cv_ln_g, cv_ln_b, cv_w_pw2, cv_b_pw2, final_g):
    import ml_dtypes
    f32 = np.float32
    TL = S // 4
    NT = TL + 64
    A = lambda a: np.asarray(a, dtype=f32)
    x, c, ctx, c_ctx = A(x), A(c), A(ctx), A(c_ctx)
    cores = [(r // 4, r % 4) for r in range(NCORE)]

    xT = [np.ascontiguousarray(np.concatenate([x[b, q * TL:(q + 1) * TL], ctx[b, q * 64:(q + 1) * 64]], 0).T) for b, q in cores]
    cT = [np.ascontiguousarray(np.stack([fm_vec(c[b]), fm_vec(c_ctx)], -1)) for b, q in cores]

    def layer_in(L):
        return {f"wmod{L}": tile_w(A(w_mod[L])), f"bmod{L}": fm_vec(A(b_mod[L])), f"normg{L}": fm_vec(A(norm_g[L]))}

    def ffn_in(L, k):
        wi, wo_ = _ffn_w(A(w_ffn_in[L, k]), A(w_ffn_out[L, k]))
        return {f"win{L}_{k}": wi, f"wout{L}_{k}": wo_}

    Ct, St = _rope_tables(S)
    ctab, stab = [], []
    for b, q in cores:
        ctab.append(np.ascontiguousarray(np.concatenate([Ct[:, q * TL:(q + 1) * TL], np.ones((128, 64), f32)], 1)))
        stab.append(np.ascontiguousarray(np.concatenate([St[:, q * TL:(q + 1) * TL], np.zeros((128, 64), f32)], 1)))

    perm = np.arange(D)
    for i in range(D):
        d = i % 64
        perm[i] = i + 16 if (d % 32) < 16 else i - 16

    def qkv_in(j):
        W = A(attn_w_qkv[j])
        Wq, Wk, Wv = W[:, :D], W[:, D:2 * D], W[:, 2 * D:]
        tl = [tile_w(Wq), tile_w(Wq[:, perm]), tile_w(Wk), tile_w(Wk[:, perm])]
        wqk = np.stack(tl, 1).reshape(32, 128, KC, 128)
        wv = np.ascontiguousarray(Wv.reshape(KC, 128, D).transpose(1, 0, 2))
        return {"wqk": np.ascontiguousarray(wqk), "wv": wv}

    def run_attn(res, j, lam_init, ctx_out):
        nc = _prog(("attn", S, j, ctx_out), lambda: build_attn(S, lam_init, ctx_out))
        maps = []
        for hd in range(8):
            QTh = np.stack([np.concatenate([res[b * 4 + q]["QT"][hd][:, :TL] for q in range(4)] +
                                           [res[b * 4 + q]["QT"][hd][:, TL:] for q in range(4)], 1) for b in range(2)])
            KTh = np.stack([np.concatenate([res[b * 4 + q]["KT"][hd][:, :TL] for q in range(4)] +
                                           [res[b * 4 + q]["KT"][hd][:, TL:] for q in range(4)], 1) for b in range(2)])
            Vh = np.stack([np.concatenate([res[b * 4 + q]["V"][:TL, hd * 128:(hd + 1) * 128] for q in range(4)] +
                                          [res[b * 4 + q]["V"][TL:, hd * 128:(hd + 1) * 128] for q in range(4)], 0) for b in range(2)])
            maps.append({"QT": np.ascontiguousarray(QTh), "KT": np.ascontiguousarray(KTh), "V": np.ascontiguousarray(Vh),
                         "lamv": A(attn_lambda[j]), "subg": A(attn_subln_g[j]).reshape(128, 1)})
        ro = _launch(nc, maps)
        oT = []
        for b, q in cores:
            o = np.zeros((D, NT), f32)
            for hd in range(8):
                o[hd * 128:(hd + 1) * 128, :TL] = ro[hd]["OT"][b][:, q * TL:(q + 1) * TL]
                if ctx_out:
                    o[hd * 128:(hd + 1) * 128, TL:] = ro[hd]["OT"][b][:, S + q * 64:S + (q + 1) * 64]
            oT.append(o)
        return oT

    ncA = _prog(("A", S), lambda: build_tok("A", S))
    shared = {}
    shared.update(layer_in(0)); shared.update(ffn_in(0, 0)); shared.update(qkv_in(0))
    resA = _launch(ncA, [dict(shared, xT=xT[r], cT=cT[r], ctab=ctab[r], stab=stab[r]) for r in range(NCORE)])
    xT = [resA[r]["xo"] for r in range(NCORE)]
    oT = run_attn(resA, 0, 0.8 - 0.6 * math.exp(-0.3 * 0), True)

    ncC = _prog(("C", S), lambda: build_tok("C", S))
    shared = {}
    shared.update(layer_in(0)); shared.update(layer_in(1)); shared.update(ffn_in(0, 1)); shared.update(ffn_in(1, 0))
    shared["wo"] = tile_w(A(attn_w_o[0]))
    shared["whin"] = tile_w(A(hy_w_in[0]))
    shared["bhin"] = fm_vec(A(hy_b_in[0]))
    resC = _launch(ncC, [dict(shared, xT=xT[r], cT=cT[r], oT=oT[r]) for r in range(NCORE)])
    xT = [resC[r]["xo"] for r in range(NCORE)]

    ncD = _prog(("D", S), lambda: build_hyena(S))
    emb_l, tex_l = _hy_tables(S)
    emb_c, tex_c = _hy_tables(CTX)
    max_decay = math.log(1e-2) / 0.3
    min_decay = math.log(1e-2) / 1.5
    deltas = np.abs(np.linspace(min_decay, max_decay, D, dtype=f32)).astype(f32)
    U = [np.concatenate([resC[b * 4 + q]["uT"][:, :TL] for q in range(4)], 1).reshape(3, D, S) for b in range(2)]
    Uc = [np.concatenate([resC[b * 4 + q]["uT"][:, TL:] for q in range(4)], 1).reshape(3, D, CTX) for b in range(2)]
    wsh_full = A(hy_w_short[0])
    bsh_full = A(hy_b_short[0])
    w4 = A(hy_f_w4[0])
    maps = []
    for r in range(NCORE):
        cs = slice(r * 128, (r + 1) * 128)
        maps.append({
            "up_l": np.ascontiguousarray(np.stack([U[b][:, cs, :] for b in range(2)])),
            "up_c": np.ascontiguousarray(np.stack([Uc[b][:, cs, :] for b in range(2)])),
            "fw1": A(hy_f_w1[0]), "fw2": A(hy_f_w2[0]), "fw3": A(hy_f_w3[0]),
            "fw4": np.ascontiguousarray(np.stack([w4[:, cs], w4[:, D + r * 128:D + (r + 1) * 128]], 1)),
            "fbf": np.ascontiguousarray(np.stack([A(hy_f_b1[0]), A(hy_f_b2[0]), A(hy_f_b3[0]), A(hy_f_freq[0])], 1)),
            "wsh": np.ascontiguousarray(wsh_full.reshape(3, 3, D)[:, :, cs].transpose(2, 1, 0)),
            "bsh": np.ascontiguousarray(bsh_full.reshape(3, D)[:, cs].T),
            "skp": A(hy_skip[0])[cs].reshape(128, 1), "dlt": deltas[cs].reshape(128, 1),
            "ident": np.eye(128, dtype=f32), "antiid": np.ascontiguousarray(np.eye(128, dtype=f32)[::-1]),
            "emb_l": emb_l, "emb_c": emb_c, "tex_l": tex_l, "tex_c": tex_c,
        })
    resD = _launch(ncD, maps)
    zT = []
    for b, q in cores:
        z = np.zeros((D, NT), f32)
        for r in range(NCORE):
            z[r * 128:(r + 1) * 128, :TL] = resD[r]["z_l"][b][:, q * TL:(q + 1) * TL]
            z[r * 128:(r + 1) * 128, TL:] = resD[r]["z_c"][b][:, q * 64:(q + 1) * 64]
        zT.append(z)

    ncE1 = _prog(("E1", S), lambda: build_tok("E1", S))
    shared = {}
    shared.update(layer_in(1)); shared.update(layer_in(2)); shared.update(ffn_in(1, 1)); shared.update(ffn_in(2, 0))
    shared["who"] = tile_w(A(hy_w_out[0]))
    shared["bho"] = fm_vec(A(hy_b_out[0]))
    po = []
    for j in range(8):
        po += [j, 8 + j]
    shared["wpw1"] = tile_w(A(cv_w_pw1[0]), po)
    shared["bpw1"] = np.ascontiguousarray(fm_vec(A(cv_b_pw1[0]))[:, po])
    resE1 = _launch(ncE1, [dict(shared, xT=xT[r], cT=cT[r], zT=zT[r]) for r in range(NCORE)])
    xT = [resE1[r]["xo"] for r in range(NCORE)]
    gH = []
    Gl = [np.pad(np.concatenate([resE1[b * 4 + q]["gT"][:, :TL] for q in range(4)], 1), ((0, 0), (15, 15))) for b in range(2)]
    Gc = [np.pad(np.concatenate([resE1[b * 4 + q]["gT"][:, TL:] for q in range(4)], 1), ((0, 0), (15, 15))) for b in range(2)]
    for b, q in cores:
        gH.append(np.ascontiguousarray(np.concatenate([Gl[b][:, q * TL:q * TL + TL + 30], Gc[b][:, q * 64:q * 64 + 94]], 1)))

    ncE2 = _prog(("E2", S), lambda: build_tok("E2", S))
    shared = {}
    shared.update(layer_in(2)); shared.update(layer_in(3)); shared.update(ffn_in(2, 1)); shared.update(ffn_in(3, 0))
    shared.update(qkv_in(1))
    shared["wdw"] = np.ascontiguousarray(fm_vec(A(cv_w_dw[0])).transpose(0, 2, 1))
    shared["bdw"] = fm_vec(A(cv_b_dw[0]))
    shared["lng"] = fm_vec(A(cv_ln_g[0]))
    shared["lnb"] = fm_vec(A(cv_ln_b[0]))
    shared["wpw2"] = tile_w(A(cv_w_pw2[0]))
    shared["bpw2"] = fm_vec(A(cv_b_pw2[0]))
    resE2 = _launch(ncE2, [dict(shared, xT=xT[r], cT=cT[r], gH=gH[r], ctab=ctab[r], stab=stab[r]) for r in range(NCORE)])
    xT = [resE2[r]["xo"] for r in range(NCORE)]
    oT = run_attn(resE2, 1, 0.8 - 0.6 * math.exp(-0.3 * 3), False)

    ncF = _prog(("F", S), lambda: build_tok("F", S))
    shared = {}
    shared.update(layer_in(3)); shared.update(ffn_in(3, 1))
    shared["wo"] = tile_w(A(attn_w_o[1]))
    shared["fg"] = fm_vec(A(final_g))
    resF = _launch(ncF, [dict(shared, xT=xT[r], cT=cT[r], oT=oT[r]) for r in range(NCORE)])
    out = np.zeros((B, S, D), f32)
    for r, (b, q) in enumerate(cores):
        out[b, q * TL:(q + 1) * TL, :] = resF[r]["yT"].T
    return out


def kernel(**inputs):
    return _forward(SEQ, **inputs)
```
